# Optimizing a Trainium2 kernel written in Bass

```python
import math
import jax, jax.numpy as jnp
from jax import lax
import numpy as np

D_MODEL = 2048
BATCH = 1
SEQ = 8192
DEPTH = 2

N_MIXERS = 2
N_ATTN = (DEPTH + 1) // 2
N_HGRN = DEPTH // 2

ATTN_HEAD_DIM = 64
ATTN_HEADS = D_MODEL // ATTN_HEAD_DIM
ATTN_KV_HEADS = ATTN_HEADS // 8
ATTN_GROUP = ATTN_HEADS // ATTN_KV_HEADS
WINDOW = 128
BLOCK = 128
ATTN_Q_DIM = ATTN_HEADS * ATTN_HEAD_DIM
ATTN_KV_DIM = ATTN_KV_HEADS * ATTN_HEAD_DIM
ATTN_IN_DIM = ATTN_Q_DIM + 2 * ATTN_KV_DIM
ATTN_SCALE = 1.0 / math.sqrt(ATTN_HEAD_DIM)

HGRN_EXPAND = 128
HGRN_HEADS = D_MODEL // HGRN_EXPAND
HGRN_DK = HGRN_EXPAND
HGRN_DV = D_MODEL // HGRN_HEADS
HGRN_FDIM = HGRN_HEADS * HGRN_DK
HGRN_IDIM = HGRN_HEADS * HGRN_DV
HGRN_IN_DIM = 2 * HGRN_FDIM + 2 * HGRN_IDIM
HGRN_SCALE = 1.0 / math.sqrt(HGRN_DK)
CHUNK = 64

D_FF = 4 * D_MODEL
N_MOD = 6
EPS = 1e-6

kernel_name = "hybrid_swa_hgrn2_block"


def rms_norm(x, gain):
    xf = x.astype(jnp.float32)
    y = xf * lax.rsqrt(jnp.mean(xf * xf, axis=-1, keepdims=True) + EPS)
    return (y * gain.astype(jnp.float32)).astype(x.dtype)


def modulate(h, shift, scale):
    return h * (1.0 + scale[:, None, :]) + shift[:, None, :]


def alibi_slopes(n_heads):
    return jnp.exp2(-8.0 * jnp.arange(1, n_heads + 1, dtype=jnp.float32) / n_heads)


def swa_sink_attention(h, w_in, w_out, q_gain, k_gain, sinks):
    B, T, _ = h.shape
    nb = T // BLOCK
    proj = h @ w_in
    q, k, v = jnp.split(proj, [ATTN_Q_DIM, ATTN_Q_DIM + ATTN_KV_DIM], axis=-1)
    q = rms_norm(q.reshape(B, T, ATTN_HEADS, ATTN_HEAD_DIM), q_gain)
    k = rms_norm(k.reshape(B, T, ATTN_KV_HEADS, ATTN_HEAD_DIM), k_gain)
    v = v.reshape(B, T, ATTN_KV_HEADS, ATTN_HEAD_DIM)
    q = q.reshape(B, nb, BLOCK, ATTN_KV_HEADS, ATTN_GROUP, ATTN_HEAD_DIM)

    def band(a):
        ap = jnp.pad(a, ((0, 0), (BLOCK, 0), (0, 0), (0, 0)))
        ap = ap.reshape(B, nb + 1, BLOCK, ATTN_KV_HEADS, ATTN_HEAD_DIM)
        return jnp.concatenate([ap[:, :-1], ap[:, 1:]], axis=2)

    kb, vb = band(k), band(v)
    logits = jnp.einsum('bnqkgd,bnskd->bnkgqs', q, kb,
                        preferred_element_type=jnp.float32) * ATTN_SCALE

    kpos = jnp.arange(2 * BLOCK)
    dist = (jnp.arange(BLOCK) + BLOCK)[:, None] - kpos[None, :]
    in_band = (dist >= 0) & (dist < WINDOW)
    s_abs = (jnp.arange(nb) * BLOCK - BLOCK)[:, None, None] + kpos[None, None, :]
    valid = in_band[None] & (s_abs >= 0)
    slopes = alibi_slopes(ATTN_HEADS).reshape(ATTN_KV_HEADS, ATTN_GROUP)
    alibi = -slopes[:, :, None, None] * jnp.abs(dist).astype(jnp.float32)
    logits = jnp.where(valid[None, :, None, None], logits + alibi, -jnp.inf)

    sink = jnp.broadcast_to(
        sinks.astype(jnp.float32).reshape(1, 1, ATTN_KV_HEADS, ATTN_GROUP, 1, 1),
        logits.shape[:-1] + (1,))
    probs = jax.nn.softmax(jnp.concatenate([logits, sink], axis=-1), axis=-1)[..., :-1]
    out = jnp.einsum('bnkgqs,bnskd->bnqkgd', probs.astype(vb.dtype), vb)
    return out.reshape(B, T, ATTN_Q_DIM) @ w_out


def hgrn2_mixer(h, w_in, w_out, o_gain, lower_bound):
    B, T, _ = h.shape
    nc = T // CHUNK
    proj = (h @ w_in).astype(jnp.float32)
    q, f, v, g = jnp.split(proj, [HGRN_FDIM, 2 * HGRN_FDIM, 2 * HGRN_FDIM + HGRN_IDIM], axis=-1)
    q = jax.nn.silu(q) * HGRN_SCALE
    forget = lower_bound + (1.0 - lower_bound) * jax.nn.sigmoid(f)
    k = 1.0 - forget
    logf = jnp.log(forget)

    def to_chunks(a, d):
        return a.reshape(B, nc, CHUNK, HGRN_HEADS, d)

    q, k, logf = to_chunks(q, HGRN_DK), to_chunks(k, HGRN_DK), to_chunks(logf, HGRN_DK)
    v = to_chunks(v, HGRN_DV)
    b = jnp.cumsum(logf, axis=2)
    piv = b[:, :, CHUNK // 2 - 1:CHUNK // 2]

    causal = jnp.tril(jnp.ones((CHUNK, CHUNK), dtype=bool))
    a = jnp.einsum('bnchd,bnshd->bnhcs', q * jnp.exp(b - piv), k * jnp.exp(piv - b))
    a = jnp.where(causal, a, 0.0)
    o_intra = jnp.einsum('bnhcs,bnshv->bnchv', a, v)

    b_last = b[:, :, -1]
    upd = jnp.einsum('bnshd,bnshv->nbhdv', k * jnp.exp(b_last[:, :, None] - b), v)
    decay = jnp.exp(b_last).transpose(1, 0, 2, 3)

    def step(state, inp):
        dec, u = inp
        return dec[..., None] * state + u, state

    s0 = jnp.zeros((B, HGRN_HEADS, HGRN_DK, HGRN_DV), jnp.float32)
    _, s_before = lax.scan(step, s0, (decay, upd))
    o_inter = jnp.einsum('bnchd,nbhdv->bnchv', q * jnp.exp(b), s_before)

    o = (o_intra + o_inter).reshape(B, T, HGRN_HEADS, HGRN_DV)
    o = rms_norm(o, o_gain) * jax.nn.silu(g).reshape(B, T, HGRN_HEADS, HGRN_DV)
    return o.reshape(B, T, HGRN_IDIM).astype(h.dtype) @ w_out


def squared_relu_mlp(h, w1, w2):
    a = jax.nn.relu(h @ w1)
    return (a * a) @ w2


def setup_inputs(seed: int = 0) -> dict:
    key = jax.random.key(seed)
    ks = jax.random.split(key, 17)
    nrm = jax.random.normal
    f32 = jnp.float32
    x = nrm(ks[0], (BATCH, SEQ, D_MODEL), f32)
    c = nrm(ks[1], (BATCH, D_MODEL), f32)
    mod_w = nrm(ks[2], (DEPTH, D_MODEL, N_MOD * D_MODEL), f32) * (0.5 * D_MODEL ** -0.5)
    mod_b = nrm(ks[3], (DEPTH, N_MOD * D_MODEL), f32) * 0.02
    norm_mix = 1.0 + 0.05 * nrm(ks[4], (DEPTH, D_MODEL), f32)
    norm_mlp = 1.0 + 0.05 * nrm(ks[5], (DEPTH, D_MODEL), f32)
    attn_w_in = nrm(ks[6], (N_ATTN, D_MODEL, ATTN_IN_DIM), f32) * D_MODEL ** -0.5
    attn_w_out = nrm(ks[7], (N_ATTN, ATTN_Q_DIM, D_MODEL), f32) * ATTN_Q_DIM ** -0.5
    attn_q_gain = 1.0 + 0.05 * nrm(ks[8], (N_ATTN, ATTN_HEAD_DIM), f32)
    attn_k_gain = 1.0 + 0.05 * nrm(ks[9], (N_ATTN, ATTN_HEAD_DIM), f32)
    attn_sinks = nrm(ks[10], (N_ATTN, ATTN_HEADS), f32)
    hgrn_w_in = nrm(ks[11], (N_HGRN, D_MODEL, HGRN_IN_DIM), f32) * D_MODEL ** -0.5
    hgrn_w_out = nrm(ks[12], (N_HGRN, HGRN_IDIM, D_MODEL), f32) * HGRN_IDIM ** -0.5
    hgrn_o_gain = 1.0 + 0.05 * nrm(ks[13], (N_HGRN, HGRN_HEADS, HGRN_DV), f32)
    hgrn_lb_logits = 0.5 * nrm(ks[14], (DEPTH, HGRN_FDIM), f32)
    mlp_w1 = nrm(ks[15], (DEPTH, D_MODEL, D_FF), f32) * D_MODEL ** -0.5
    mlp_w2 = nrm(ks[16], (DEPTH, D_FF, D_MODEL), f32) * D_FF ** -0.5
    return {"x": x, "c": c, "mod_w": mod_w, "mod_b": mod_b,
            "norm_mix": norm_mix, "norm_mlp": norm_mlp,
            "attn_w_in": attn_w_in, "attn_w_out": attn_w_out,
            "attn_q_gain": attn_q_gain, "attn_k_gain": attn_k_gain, "attn_sinks": attn_sinks,
            "hgrn_w_in": hgrn_w_in, "hgrn_w_out": hgrn_w_out, "hgrn_o_gain": hgrn_o_gain,
            "hgrn_lb_logits": hgrn_lb_logits, "mlp_w1": mlp_w1, "mlp_w2": mlp_w2}


def reference(x, c, mod_w, mod_b, norm_mix, norm_mlp, attn_w_in, attn_w_out,
              attn_q_gain, attn_k_gain, attn_sinks, hgrn_w_in, hgrn_w_out, hgrn_o_gain,
              hgrn_lb_logits, mlp_w1, mlp_w2):
    lb_p = jax.nn.softmax(hgrn_lb_logits.astype(jnp.float32), axis=0)
    lower_bounds = jnp.cumsum(lb_p, axis=0) - lb_p[0]
    cond = jax.nn.silu(c)
    for i in range(DEPTH):
        mod = cond @ mod_w[i] + mod_b[i]
        sh1, sc1, g1, sh2, sc2, g2 = jnp.split(mod, N_MOD, axis=-1)
        h = modulate(rms_norm(x, norm_mix[i]), sh1, sc1)
        j = i // N_MIXERS
        if i % N_MIXERS == 0:
            y = swa_sink_attention(h, attn_w_in[j], attn_w_out[j], attn_q_gain[j],
                                   attn_k_gain[j], attn_sinks[j])
        else:
            y = hgrn2_mixer(h, hgrn_w_in[j], hgrn_w_out[j], hgrn_o_gain[j], lower_bounds[i])
        x = x + g1[:, None, :] * y
        h = modulate(rms_norm(x, norm_mlp[i]), sh2, sc2)
        x = x + g2[:, None, :] * squared_relu_mlp(h, mlp_w1[i], mlp_w2[i])
    return x
```

```python
import numpy as np
from contextlib import ExitStack
import concourse.bass as bass
import concourse.mybir as mybir
from concourse.bass_utils import run_bass_kernel_spmd

F32 = mybir.dt.float32
BF16 = mybir.dt.bfloat16
AF = mybir.ActivationFunctionType
ALU = mybir.AluOpType


class _Op:
    __slots__ = ("eng", "fn", "kind", "sem", "pos", "waits", "signal", "val", "idx")


class Sched:
    ENGS = ("pe", "act", "dve", "pool", "sp")

    def __init__(self, nc):
        self.nc = nc
        self.stack = ExitStack()
        self.ops = {e: [] for e in self.ENGS}
        self.last_writer = {}
        self.readers = {}
        self.waited = {e: {} for e in self.ENGS}
        self.dma_count = {}
        self.total_sems = set()
        self.n = 0

    def sbuf(self, name, shape, dtype):
        return self.stack.enter_context(self.nc.sbuf_tensor("sb_" + name, list(shape), dtype))

    def psum(self, name, shape, dtype=F32):
        return self.stack.enter_context(self.nc.psum_tensor("ps_" + name, list(shape), dtype))

    def _deps(self, op, reads, writes):
        deps = []
        for k in reads:
            w = self.last_writer.get(k)
            if w is not None:
                deps.append(w)
        for k in writes:
            rd = self.readers.get(k, ())
            if rd:
                deps.extend(rd)
            else:
                w = self.last_writer.get(k)
                if w is not None:
                    deps.append(w)
        waits = {}
        for d in deps:
            if d is op:
                continue
            if d.kind == "dma":
                key = ("dma", d.sem)
                v = d.val
            else:
                if d.eng == op.eng and op.kind != "dma" and d.eng == "pe":
                    continue
                key = ("eng", d.eng)
                v = d.pos
            if self.waited[op.eng].get(key, -1) >= v:
                continue
            if waits.get(key, (-1, None))[0] < v:
                waits[key] = (v, d)
        for key, (v, d) in waits.items():
            self.waited[op.eng][key] = v
            d.signal = True
        op.waits = list(waits.items())
        for k in writes:
            self.last_writer[k] = op
            self.readers[k] = []
        for k in reads:
            self.readers.setdefault(k, []).append(op)

    def op(self, eng, fn, reads=(), writes=()):
        o = _Op()
        o.eng, o.fn, o.kind, o.sem = eng, fn, "cmp", None
        o.pos = len(self.ops[eng])
        o.signal = False
        o.val = None
        o.idx = self.n
        self.n += 1
        self._deps(o, list(reads), list(writes))
        self.ops[eng].append(o)
        return o

    def dma(self, eng, out, in_, sem, reads=(), writes=(), **kw):
        o = _Op()
        o.eng, o.kind, o.sem = eng, "dma", sem
        o.fn = lambda e: e.dma_start(out=out, in_=in_, **kw)
        o.pos = len(self.ops[eng])
        o.signal = True
        self.dma_count[sem] = self.dma_count.get(sem, 0) + 16
        o.val = self.dma_count[sem]
        o.idx = self.n
        self.n += 1
        self._deps(o, list(reads), list(writes))
        self.ops[eng].append(o)
        return o

    def finish(self, final_waits=()):
        nc = self.nc
        st = self.stack
        esem = {e: st.enter_context(nc.semaphore("sem_" + e)) for e in self.ENGS}
        dsem = {s: st.enter_context(nc.semaphore("dsem_" + s)) for s in self.dma_count}
        for e in self.ENGS:
            c = 0
            for o in self.ops[e]:
                if o.kind == "cmp" and o.signal:
                    c += 1
                    o.val = c
        block = st.enter_context(nc.Block())
        engobj = {"pe": "tensor", "act": "scalar", "dve": "vector", "pool": "gpsimd", "sp": "sync"}

        def emit(ename):
            def body(eng):
                for o in self.ops[ename]:
                    for key, (v, d) in o.waits:
                        if key[0] == "dma":
                            if key[1] in self.total_sems:
                                v = self.dma_count[key[1]]
                            eng.wait_ge(dsem[key[1]], v)
                        else:
                            eng.wait_ge(esem[key[1]], d.val)
                    ins = o.fn(eng)
                    if o.kind == "dma":
                        ins.then_inc(dsem[o.sem], 16)
                    elif o.signal:
                        ins.then_inc(esem[ename], 1)
                if ename == "sp":
                    for s in final_waits:
                        eng.wait_ge(dsem[s], self.dma_count[s])
            return body

        for ename in self.ENGS:
            getattr(block, engobj[ename])(emit(ename))
        st.close()


NCORES = 8
SEQ = 8192
D = 2048
NT = SEQ // NCORES
KC = D // 128
TW = 512
NTT = NT // TW
DFF = 4 * D
EPS = 1e-6
NEG = -30000.0
NSLOT = 2

C_C = 0
C_L = [16, 16 + 128]
C_LB = 16 + 256
C_OG = C_LB + 32
C_QG = C_OG + 16
C_KG = C_QG + 1
C_SINK = C_KG + 1
C_HB = C_SINK + 16
C_MS = C_HB + 1
C_OMS = C_MS + 8
NV = C_OMS + 8
HSCALE = 1.0 / float(np.sqrt(128.0))


def _fm(v):
    return np.ascontiguousarray(np.asarray(v, np.float32).reshape(-1, 128).T)


class Builder:
    def __init__(self, layers=(0, 1), debug=None, hpass=2):
        self.layers = layers
        self.debug = debug
        self.hpass = hpass
        nc = self.nc = bass.Bass("TRN2", target_bir_lowering=False)
        self.S = S = Sched(nc)
        dr = lambda n, s, k="ExternalInput": nc.dram_tensor(n, list(s), F32, kind=k).ap()
        self.xT_d = dr("xT", [D, NT])
        if 0 in layers:
            self.xh_d = dr("xh", [D, 128])
        self.vec_d = dr("vecs", [128, NV])
        if 0 in layers:
            self.dmat_d = dr("dmat", [128, 256])
        else:
            self.cst2_d = dr("cst2", [128, 64 + 128 + 512])
        self.mod_w = dr("mod_w", [len(layers), D, 6 * D])
        if debug != "nomlp" and not (1 in layers and hpass == 1):
            self.mlp_w1 = dr("mlp_w1", [2, D, DFF])
            self.mlp_w2 = dr("mlp_w2", [2, DFF, D])
        if 0 in layers:
            self.attn_w_in = dr("attn_w_in", [D, 2560])
            self.attn_w_out = dr("attn_w_out", [D, D])
        if 1 in layers:
            self.hgrn_w_in = dr("hgrn_w_in", [D, 4 * D])
            if hpass == 2:
                self.hgrn_w_out = dr("hgrn_w_out", [D, D])
        if 1 in layers and hpass == 1:
            self.U_d = dr("U_out", [128, 2048], "ExternalOutput")
            self.dt_d = dr("dt_out", [128, 16], "ExternalOutput")
        else:
            self.out_d = dr("outT", [D, NT], "ExternalOutput")
        if 1 in layers and hpass == 2:
            self.Uall_d = dr("Uall", [NCORES, 128, 2048])
            self.dall_d = dr("dall", [128, NCORES * 16])

        self.x = S.sbuf("x", [128, KC, NT], F32)
        self.h = S.sbuf("h", [128, KC, NT], BF16)
        self.wslot = [S.sbuf("w%d" % i, [128, 8192], BF16) for i in range(NSLOT)]
        self.mid = [S.sbuf("mid%d" % i, [128, 4, NT], BF16) for i in range(2)]
        self.vec = S.sbuf("vec", [128, NV], F32)
        self.modT = S.sbuf("modT", [128, 2 * 96], F32)
        self.coef = S.sbuf("coef", [128, 2 * 96], F32)
        self.ones_bf = S.sbuf("ones_bf", [128, 128], BF16)
        self.blk_bf = S.sbuf("blk_bf", [128, 128], BF16)
        self.one_f = S.sbuf("one_f", [1, 1], F32)
        self.cond = S.sbuf("cond", [128, KC], BF16)
        self.sq = [S.sbuf("sq%d" % i, [128, TW], BF16) for i in range(2)]
        self.rstd = [S.sbuf("rstd%d" % i, [128, TW], F32) for i in range(2)]
        self.epsv = S.sbuf("epsv", [128, 1], F32)
        self.NHF = 4 if 0 in layers else 10
        self.hf = [S.sbuf("hf%d" % i, [128, TW], F32) for i in range(self.NHF)]
        if 0 in layers:
            self.bank = [S.psum("bank%d" % i, [128, TW], F32) for i in range(8)]
            self.xh = S.sbuf("xh", [128, KC, 128], F32)
            self.hh = S.sbuf("hh", [128, KC, 128], BF16)
            self.dmat = S.sbuf("dmat", [128, 256], F32)
            self.esink = S.sbuf("esink", [128, KC], F32)
            self.qg8 = S.sbuf("qg8", [128, 1], F32)
            self.kT = S.sbuf("kT", [128, 4, NT + 128], BF16)
            self.vtm = S.sbuf("vtm", [128, 9, 256], BF16)
            self.pT = [S.sbuf("pT%d" % i, [128, 256], BF16) for i in range(3)]
            self.tmpa = [S.sbuf("tmpa%d" % i, [128, 256], F32) for i in range(3)]
            self.MODB, self.MODT = (4, 5), 7
        else:
            self.bank = [S.psum("bank%d" % i, [128, TW], F32) for i in range(7)]
            self.pst = S.psum("pst", [128, 1024], BF16)
            self.MODB, self.MODT = (0, 1), 6
            self.cst2 = S.sbuf("cst2", [128, 64 + 128 + 512], F32)
            self.identb = S.sbuf("identb", [128, 128], BF16)
            self.vth = S.sbuf("vth", [128, 8, 128], BF16)
            self.Sf = S.sbuf("Sf", [128, 16, 128], F32)
            self.Sb = S.sbuf("Sb", [128, 16, 128], BF16)
            self.klT = [S.sbuf("klT%d" % i, [128, 128], BF16) for i in range(2)]
            self.At = [S.sbuf("At%d" % i, [128, 64], BF16) for i in range(2)]
            self.hb = [S.sbuf("hb%d" % i, [128, TW], BF16) for i in range(6)]
            self.dec = [S.sbuf("dec%d" % i, [128, 8], F32) for i in range(2)]
            self.lsum = S.sbuf("lsum", [128, 32], F32)
            self.lbv = S.sbuf("lbv", [128, 48], F32)
            if hpass == 2:
                self.dall = S.sbuf("dall", [128, NCORES * 16], F32)
                self.avec = S.sbuf("avec", [128, 16], F32)
        self.slot_i = 0
        self.cnt = {}

    def rot(self, name, n):
        i = self.cnt.get(name, 0)
        self.cnt[name] = i + 1
        return i % n

    def wload(self, kind, src, ncols=512):
        S = self.S
        s = self.slot_i % NSLOT
        self.slot_i += 1
        slot = self.wslot[s]
        if kind == "kn":
            v = slot[:, 0:16 * ncols].rearrange("p (k n) -> p k n", n=ncols)
            sv = src.rearrange("(k p) n -> p k n", p=128)
            for part in range(4):
                S.dma("pool", v[:, part * 4:(part + 1) * 4, :], sv[:, part * 4:(part + 1) * 4, :],
                      sem="w%d" % s, writes=[("w", s, 2 * part), ("w", s, 2 * part + 1)])
        else:
            v = slot[:, :].rearrange("p (k n) -> p k n", n=2048)
            sv = src.rearrange("(k p) n -> p k n", p=128)
            for part in range(4):
                S.dma("pool", v[:, part:part + 1, :], sv[:, part:part + 1, :],
                      sem="w%d" % s, writes=[("w", s, 2 * part), ("w", s, 2 * part + 1)])
        return s, v

    def wkeys(self, s):
        return [("w", s, p) for p in range(8)]

    def prologue(self):
        S = self.S
        S.total_sems.add("cst")
        S.total_sems.add("xin")
        S.dma("sp", self.vec[:], self.vec_d, sem="cst", writes=["vec"])
        xs = self.xT_d.rearrange("(k p) t -> p k t", p=128)
        for c in range(KC):
            S.dma("sp", self.x[:, c, :], xs[:, c, :], sem="xin", writes=[("x", c, 0), ("x", c, 1)])
        S.op("dve", lambda e: e.memset(self.epsv[:], EPS), writes=["epsv"])
        if 0 in self.layers:
            xhs = self.xh_d.rearrange("(k p) t -> p k t", p=128)
            S.dma("sp", self.xh[:], xhs, sem="xin", writes=[("xh", c, 0) for c in range(KC)])
            S.dma("sp", self.dmat[:], self.dmat_d, sem="cst", writes=["dmat"])
            S.op("act", lambda e: e.activation(self.esink[:], self.vec[:, C_SINK:C_SINK + 16], AF.Exp),
                 reads=["vec"], writes=["esink"])
            S.op("dve", lambda e: e.tensor_scalar(self.qg8[:], self.vec[:, C_QG:C_QG + 1], 0.125, None, ALU.mult),
                 reads=["vec"], writes=["qg8"])
        else:
            S.dma("sp", self.cst2[:], self.cst2_d, sem="cst", writes=["cst2"])
            S.op("dve", lambda e: e.tensor_copy(self.identb[:], self.cst2[:, 64:192]), reads=["cst2"], writes=["identb"])
            S.op("dve", lambda e: e.tensor_tensor(self.lbv[:, 0:16], self.vec[:, C_LB + 16:C_LB + 32],
                                                  self.vec[:, C_LB:C_LB + 16], ALU.subtract), reads=["vec"], writes=["lbv0"])
            S.op("act", lambda e: e.activation(self.lbv[:, 0:16], self.lbv[:, 0:16], AF.Sigmoid), reads=["lbv0"], writes=["lbv0"])
            S.op("dve", lambda e: e.tensor_scalar(self.lbv[:, 16:32], self.lbv[:, 0:16], -1.0, 1.0, ALU.mult, ALU.add),
                 reads=["lbv0"], writes=["lbv1"])
            S.op("dve", lambda e: e.tensor_scalar(self.lbv[:, 32:48], self.lbv[:, 0:16], 1.0, -1.0, ALU.mult, ALU.add),
                 reads=["lbv0"], writes=["lbv2"])
        S.op("dve", lambda e: e.memset(self.ones_bf[:], 1.0), writes=["ones_bf"])
        S.op("dve", lambda e: e.memset(self.blk_bf[:], 0.0), writes=["blk_bf"])
        S.op("dve", lambda e: e.memset(self.blk_bf[0:64, 0:64], 1.0), reads=["blk_bf"], writes=["blk_bf"])
        S.op("dve", lambda e: e.memset(self.blk_bf[64:128, 64:128], 1.0), reads=["blk_bf"], writes=["blk_bf"])
        S.op("dve", lambda e: e.memset(self.one_f[:], 1.0), writes=["one_f"])
        S.op("act", lambda e: e.activation(self.cond[:], self.vec[:, C_C:C_C + 16], AF.Silu),
             reads=["vec"], writes=["cond"])

    def mod_layer(self, l):
        S = self.S
        pT = self.MODT
        for nb in range(24):
            s, wv = self.wload("kn", self.mod_w[self.layers.index(l), :, nb * 512:(nb + 1) * 512])
            b = self.MODB[self.rot("modps", 2)]
            for k in range(KC):
                S.op("pe", lambda e, k=k, b=b, wv=wv: e.matmul(self.bank[b][0:1, :], self.cond[:, k:k + 1], wv[:, k, :],
                                                         start=(k == 0), stop=(k == KC - 1)),
                     reads=["cond"] + self.wkeys(s), writes=[("ps", b)])
            r = self.rot("hf", self.NHF)
            S.op("act", lambda e, b=b, r=r: e.activation(self.hf[r][0:1, :], self.bank[b][0:1, :], AF.Identity),
                 reads=[("ps", b)], writes=[("hf", r)])
            for j in range(4):
                col = nb * 4 + j
                S.op("pe", lambda e, r=r, j=j, col=col: e.matmul(self.bank[pT][:, col:col + 1],
                                                                self.hf[r][0:1, j * 128:(j + 1) * 128],
                                                                self.one_f[0:1, 0:1], start=True, stop=True),
                     reads=[("hf", r), "one_f"], writes=[("ps", pT)])
        mT = self.modT[:, l * 96:(l + 1) * 96]
        S.op("dve", lambda e: e.tensor_tensor(mT, self.bank[pT][:, 0:96], self.vec[:, C_L[l] + 32:C_L[l] + 128], ALU.add),
             reads=[("ps", pT), "vec"], writes=[("modT", l)])
        cf = self.coef[:, l * 96:(l + 1) * 96]
        for (dst, src_scale, nrm) in ((0, 16, C_L[l]), (48, 64, C_L[l] + 16)):
            S.op("dve", lambda e, dst=dst, src_scale=src_scale, nrm=nrm: e.scalar_tensor_tensor(
                cf[:, dst:dst + 16], mT[:, src_scale:src_scale + 16], 1.0, self.vec[:, nrm:nrm + 16], ALU.add, ALU.mult),
                reads=[("modT", l), "vec"], writes=[("coef", l, dst)])
        for (dst, src) in ((16, 0), (32, 32), (64, 48), (80, 80)):
            S.op("dve", lambda e, dst=dst, src=src: e.tensor_copy(cf[:, dst:dst + 16], mT[:, src:src + 16]),
                 reads=[("modT", l)], writes=[("coef", l, dst)])

    def coefk(self, l):
        return [("coef", l, d) for d in (0, 16, 32, 48, 64, 80)]

    def prenorm(self, l, which, src=None, dst=None, ntok=NT, keyx="x", keyh="h"):
        S = self.S
        src = self.x if src is None else src
        dst = self.h if dst is None else dst
        cf = self.coef[:, l * 96:(l + 1) * 96]
        ao, bo = (0, 16) if which == 0 else (48, 64)
        ntt = (ntok + TW - 1) // TW
        for t in range(ntt):
            w = min(TW, ntok - t * TW)
            tok = slice(t * TW, t * TW + w)
            pb = 6
            for c in range(KC):
                q = self.rot("sq", 2)
                S.op("act", lambda e, c=c, q=q, tok=tok, w=w: e.activation(self.sq[q][:, 0:w], src[:, c, tok], AF.Square),
                     reads=[(keyx, c, t)], writes=[("sq", q)])
                S.op("pe", lambda e, c=c, q=q, w=w: e.matmul(self.bank[pb][:, 0:w], self.ones_bf[:], self.sq[q][:, 0:w],
                                                        start=(c == 0), stop=(c == KC - 1)),
                     reads=[("sq", q), "ones_bf"], writes=[("ps", pb)])
            r = self.rot("rstd", 2)
            S.op("act", lambda e, r=r, w=w: e.activation(self.rstd[r][:, 0:w], self.bank[pb][:, 0:w], AF.Sqrt,
                                                    bias=self.epsv[:, 0:1], scale=1.0 / D),
                 reads=[("ps", pb), "epsv"], writes=[("rstd", r)])
            S.op("dve", lambda e, r=r, w=w: e.reciprocal(self.rstd[r][:, 0:w], self.rstd[r][:, 0:w]),
                 reads=[("rstd", r)], writes=[("rstd", r)])
            for c in range(KC):
                f = self.rot("hf", self.NHF)
                S.op("dve", lambda e, c=c, f=f, r=r, tok=tok, w=w: e.scalar_tensor_tensor(
                    self.hf[f][:, 0:w], src[:, c, tok], cf[:, ao + c:ao + c + 1], self.rstd[r][:, 0:w], ALU.mult, ALU.mult),
                    reads=[(keyx, c, t), ("rstd", r)] + self.coefk(l), writes=[("hf", f)])
                S.op("act", lambda e, c=c, f=f, tok=tok, w=w: e.activation(dst[:, c, tok], self.hf[f][:, 0:w], AF.Identity,
                                                                   bias=cf[:, bo + c:bo + c + 1]),
                     reads=[("hf", f)] + self.coefk(l), writes=[(keyh, c, t)])

    def stage2(self, l, gate_off, src_rows, midi):
        S = self.S
        s, wv = self.wload("rows", src_rows)
        cf = self.coef[:, l * 96:(l + 1) * 96]
        mid = self.mid[midi]
        for n in range(KC):
            for t in range(NTT):
                b = self.rot("projps", 4)
                for j in range(4):
                    S.op("pe", lambda e, n=n, t=t, j=j, b=b, wv=wv: e.matmul(
                        self.bank[b][:], wv[:, j, n * 128:(n + 1) * 128], mid[:, j, t * TW:(t + 1) * TW],
                        start=(j == 0), stop=(j == 3)),
                        reads=self.wkeys(s) + [("mid", midi, j, t)], writes=[("ps", b)])
                S.op("dve", lambda e, n=n, t=t, b=b: e.scalar_tensor_tensor(
                    self.x[:, n, t * TW:(t + 1) * TW], self.bank[b][:], cf[:, gate_off + n:gate_off + n + 1],
                    self.x[:, n, t * TW:(t + 1) * TW], ALU.mult, ALU.add),
                    reads=[("ps", b), ("x", n, t)] + self.coefk(l), writes=[("x", n, t)])

    def qknorm(self, b, w, gain_ap, gain_key, out_ap, out_key):
        S = self.S
        pb = 6
        q = self.rot("sq", 2)
        S.op("act", lambda e: e.activation(self.sq[q][:, 0:w], self.bank[b][:, 0:w], AF.Square),
             reads=[("ps", b)], writes=[("sq", q)])
        S.op("pe", lambda e: e.matmul(self.bank[pb][:, 0:w], self.blk_bf[:], self.sq[q][:, 0:w], start=True, stop=True),
             reads=[("sq", q), "blk_bf"], writes=[("ps", pb)])
        r = self.rot("rstd", 2)
        S.op("act", lambda e: e.activation(self.rstd[r][:, 0:w], self.bank[pb][:, 0:w], AF.Sqrt,
                                           bias=self.epsv[:, 0:1], scale=1.0 / 64),
             reads=[("ps", pb), "epsv"], writes=[("rstd", r)])
        S.op("dve", lambda e: e.reciprocal(self.rstd[r][:, 0:w], self.rstd[r][:, 0:w]),
             reads=[("rstd", r)], writes=[("rstd", r)])
        S.op("dve", lambda e: e.scalar_tensor_tensor(out_ap, self.bank[b][:, 0:w], gain_ap, self.rstd[r][:, 0:w],
                                                     ALU.mult, ALU.mult),
             reads=[("ps", b), ("rstd", r), gain_key], writes=[out_key])

    def attention(self):
        S = self.S
        l = 0
        self.prenorm(l, 0)
        self.prenorm(l, 0, src=self.xh, dst=self.hh, ntok=128, keyx="xh", keyh="hh")
        hsrc = [(self.hh, "hh", 0, 0, 128)] + [(self.h, "h", t, t * TW, TW) for t in range(NTT)]
        s = self.slot_i % NSLOT
        self.slot_i += 1
        wv = self.wslot[s][:, :].rearrange("p (k n) -> p k n", n=512)
        for j in range(4):
            for r in range(2):
                src = self.attn_w_in[:, 2048 + j * 64:2048 + (j + 1) * 64].rearrange("(k p) n -> p k n", p=128)
                S.dma("pool", wv[:, :, j * 128 + r * 64:j * 128 + (r + 1) * 64], src, sem="w%d" % s,
                      writes=[("w", s, j * 2 + r)])
        for j in range(4):
            for (buf, key, t, off, w) in hsrc:
                b = self.rot("projps", 4)
                for k in range(KC):
                    S.op("pe", lambda e, j=j, k=k, b=b, buf=buf, off=off, w=w: e.matmul(
                        self.bank[b][:, 0:w], wv[:, k, j * 128:(j + 1) * 128], buf[:, k, off:off + w],
                        start=(k == 0), stop=(k == KC - 1)),
                        reads=self.wkeys(s) + [(key, k, t)], writes=[("ps", b)])
                ko = 0 if key == "hh" else 128 + off
                self.qknorm(b, w, self.vec[:, C_KG:C_KG + 1], "vec", self.kT[:, j, ko:ko + w], ("kT", j, ko))
        kT_keys = [("kT", j, ko) for j in range(4) for ko in (0, 128, 128 + TW)]
        s, wv = self.wload("kn", self.attn_w_in[:, 2304:2560], ncols=256)
        for blk in range(9):
            b = self.rot("projps", 4)
            for k in range(KC):
                if blk == 0:
                    lhs, key = self.hh[:, k, 0:128], ("hh", k, 0)
                else:
                    lhs, key = self.h[:, k, (blk - 1) * 128:blk * 128], ("h", k, (blk - 1) // 4)
                S.op("pe", lambda e, k=k, b=b, lhs=lhs, wv=wv: e.matmul(self.bank[b][:, 0:256], lhs, wv[:, k, :],
                                                                 start=(k == 0), stop=(k == KC - 1)),
                     reads=self.wkeys(s) + [key], writes=[("ps", b)])
            S.op("act", lambda e, b=b, blk=blk: e.activation(self.vtm[:, blk, :], self.bank[b][:, 0:256], AF.Identity),
                 reads=[("ps", b)], writes=[("vtm", blk)])
        PO, PD, PST = 4, 5, 7
        for g in range(4):
            s, wv = self.wload("kn", self.attn_w_in[:, g * 512:(g + 1) * 512])
            qi = self.rot("mid", 2)
            midq = self.mid[qi]
            for j in range(4):
                for t in range(NTT):
                    b = self.rot("projps", 4)
                    for k in range(KC):
                        S.op("pe", lambda e, j=j, t=t, k=k, b=b, wv=wv: e.matmul(
                            self.bank[b][:], wv[:, k, j * 128:(j + 1) * 128], self.h[:, k, t * TW:(t + 1) * TW],
                            start=(k == 0), stop=(k == KC - 1)),
                            reads=self.wkeys(s) + [("h", k, t)], writes=[("ps", b)])
                    self.qknorm(b, TW, self.qg8[:, 0:1], "qg8", midq[:, j, t * TW:(t + 1) * TW], ("mid", qi, j, t))
            oi = self.rot("mid", 2)
            mido = self.mid[oi]
            for j in range(4):
                for half in range(2):
                    n0 = half * 4
                    for r in range(2):
                        head = 8 * g + 2 * j + r
                        slope = float(2.0 ** (-8.0 * (head + 1) / 32.0))
                        p0 = r * 64
                        for kb in range(n0, n0 + 5):
                            qlo = kb - 1 if kb - 1 >= n0 else kb
                            qhi = kb if kb <= n0 + 3 else kb - 1
                            c0 = 0 if qlo == kb - 1 else 128
                            wd = (qhi - qlo + 1) * 128
                            sti = (7, 6)[self.rot("st", 2)]
                            st = self.bank[sti][:, 0:wd]
                            S.op("pe", lambda e, st=st, j=j, kb=kb, qlo=qlo, qhi=qhi, p0=p0, midq=midq, g=g: e.matmul(
                                st, self.kT[p0:p0 + 64, g, kb * 128:(kb + 1) * 128],
                                midq[p0:p0 + 64, j, qlo * 128:(qhi + 1) * 128], start=True, stop=True),
                                reads=kT_keys + [("mid", qi, j, 0), ("mid", qi, j, 1)], writes=[("ps", sti)])
                            ai = self.rot("tmpa", 3)
                            S.op("dve", lambda e, st=st, ai=ai, c0=c0, wd=wd, slope=slope: e.scalar_tensor_tensor(
                                self.tmpa[ai][:, 0:wd], self.dmat[:, c0:c0 + wd], slope, st, ALU.mult, ALU.add),
                                reads=[("ps", sti), "dmat"], writes=[("tmpa", ai)])
                            pi = self.rot("pT", 3)
                            if kb == 0:
                                S.op("act", lambda e, ai=ai, pi=pi, wd=wd: e.activation(
                                    self.pT[pi][:, 0:wd], self.tmpa[ai][:, 0:wd], AF.Exp, bias=self.vec[:, C_HB:C_HB + 1]),
                                    reads=[("tmpa", ai), "vec"], writes=[("pT", pi)])
                            else:
                                S.op("act", lambda e, ai=ai, pi=pi, wd=wd: e.activation(
                                    self.pT[pi][:, 0:wd], self.tmpa[ai][:, 0:wd], AF.Exp),
                                    reads=[("tmpa", ai)], writes=[("pT", pi)])
                            for n in range(qlo, qhi + 1):
                                off = (n - qlo) * 128
                                oc = (n - n0) * 128
                                S.op("pe", lambda e, pi=pi, off=off, oc=oc, kb=kb, n=n, p0=p0, g=g: e.matmul(
                                    self.bank[PO][p0:p0 + 64, oc:oc + 128], self.vtm[:, kb, g * 64:(g + 1) * 64],
                                    self.pT[pi][:, off:off + 128], start=(kb == n), stop=(kb == n + 1)),
                                    reads=[("pT", pi), ("vtm", kb)], writes=[("ps", PO)])
                                S.op("pe", lambda e, pi=pi, off=off, oc=oc, kb=kb, n=n, p0=p0: e.matmul(
                                    self.bank[PD][p0:p0 + 64, oc:oc + 128], self.ones_bf[:, 0:64],
                                    self.pT[pi][:, off:off + 128], start=(kb == n), stop=(kb == n + 1)),
                                    reads=[("pT", pi), "ones_bf"], writes=[("ps", PD)])
                    f = self.rot("hf", self.NHF)
                    ch = 4 * g + j
                    S.op("dve", lambda e, f=f, ch=ch: e.tensor_scalar(self.hf[f][:], self.bank[PD][:],
                                                                     self.esink[:, ch:ch + 1], None, ALU.add),
                         reads=[("ps", PD), "esink"], writes=[("hf", f)])
                    S.op("dve", lambda e, f=f: e.reciprocal(self.hf[f][:], self.hf[f][:]),
                         reads=[("hf", f)], writes=[("hf", f)])
                    S.op("dve", lambda e, f=f, j=j, half=half, mido=mido: e.tensor_tensor(
                        mido[:, j, half * TW:(half + 1) * TW], self.bank[PO][:], self.hf[f][:], ALU.mult),
                        reads=[("ps", PO), ("hf", f)], writes=[("mid", oi, j, half)])
            self.stage2(l, 32, self.attn_w_out[g * 512:(g + 1) * 512, :], oi)

    def hgrn(self):
        S = self.S
        l = 1
        p2 = self.hpass == 2
        W = self.hgrn_w_in
        cm64 = self.cst2[:, 0:64]
        smask = self.cst2[:, 192:704]
        lb, oml, noml = self.lbv[:, 0:16], self.lbv[:, 16:32], self.lbv[:, 32:48]
        lbk = ["lbv0", "lbv1", "lbv2"]
        self.prenorm(l, 0)
        S.op("dve", lambda e: e.memset(self.Sf[:], 0.0), writes=[("Sf", hd) for hd in range(16)])
        if p2:
            S.dma("sp", self.dall[:], self.dall_d, sem="cst", writes=["dall"])
            for jc in range(NCORES - 1):
                mj = self.vec[:, C_MS + jc:C_MS + jc + 1]
                omj = self.vec[:, C_OMS + jc:C_OMS + jc + 1]
                S.op("dve", lambda e, jc=jc, mj=mj, omj=omj: e.tensor_scalar(
                    self.avec[:], self.dall[:, jc * 16:(jc + 1) * 16], mj, omj, ALU.mult, ALU.add),
                    reads=["dall", "vec"], writes=["avec"])
                for pc in range(4):
                    ui = self.rot("hf", self.NHF)
                    ub = self.hf[ui]
                    S.dma("sp", ub[:], self.Uall_d[jc, :, pc * 512:(pc + 1) * 512], sem="ust%d" % ui, writes=[("hf", ui)])
                    S.op("dve", lambda e, mj=mj, ub=ub: e.tensor_scalar(ub[:], ub[:], mj, None, ALU.mult),
                         reads=[("hf", ui), "vec"], writes=[("hf", ui)])
                    for hh_ in range(4):
                        hd = pc * 4 + hh_
                        S.op("dve", lambda e, hd=hd, hh_=hh_, ub=ub: e.scalar_tensor_tensor(
                            self.Sf[:, hd, :], self.Sf[:, hd, :], self.avec[:, hd:hd + 1], ub[:, hh_ * 128:(hh_ + 1) * 128],
                            ALU.mult, ALU.add), reads=[("Sf", hd), "avec", ("hf", ui)], writes=[("Sf", hd)])
        for hd in range(16):
            S.op("act", lambda e, hd=hd: e.activation(self.Sb[:, hd, :], self.Sf[:, hd, :], AF.Identity),
                 reads=[("Sf", hd)], writes=[("Sb", hd)])
        PSO = 4

        def proj(wv, s, j, t):
            b = self.rot("projps3", 3)
            for k in range(KC):
                S.op("pe", lambda e, k=k: e.matmul(self.bank[b][:], wv[:, k, :],
                                                   self.h[:, k, t * TW:(t + 1) * TW], start=(k == 0), stop=(k == KC - 1)),
                     reads=self.wkeys(s) + [("h", k, t)], writes=[("ps", b)])
            return b

        def hf():
            i = self.rot("hf", self.NHF)
            return self.hf[i], ("hf", i)

        def hb():
            i = self.rot("hb", 6)
            return self.hb[i], ("hb", i)

        for hg in range(4):
            if p2:
                midi = self.rot("mid", 2)
                mid = self.mid[midi]
            for j in range(4):
                hd = 4 * hg + j
                sw = self.slot_i % NSLOT
                self.slot_i += 1
                wh = self.wslot[sw][:, :].rearrange("p (k n) -> p k n", n=512)
                for qi_, cb in enumerate((2048, 0, 6144, 4096)):
                    if (not p2) and qi_ in (1, 2):
                        continue
                    S.dma("pool", wh[:, :, qi_ * 128:(qi_ + 1) * 128],
                          W[:, cb + hd * 128:cb + (hd + 1) * 128].rearrange("(k p) n -> p k n", p=128),
                          sem="w%d" % sw, writes=[("w", sw, 2 * qi_), ("w", sw, 2 * qi_ + 1)])
                wvf, wvq, wvg = wh[:, :, 0:128], wh[:, :, 128:256], wh[:, :, 256:384]
                sf = sq_ = sw
                self._wg = (sw, wvg)
                for blk in range(8):
                    b = self.rot("projps3", 3)
                    for k in range(KC):
                        S.op("pe", lambda e, k=k, b=b, blk=blk, wh=wh: e.matmul(
                            self.bank[b][:, 0:128], self.h[:, k, blk * 128:(blk + 1) * 128], wh[:, k, 384:512],
                            start=(k == 0), stop=(k == KC - 1)),
                            reads=self.wkeys(sw) + [("h", k, blk // 4)], writes=[("ps", b)])
                    S.op("act", lambda e, b=b, blk=blk: e.activation(self.vth[:, blk, :], self.bank[b][:, 0:128], AF.Identity),
                         reads=[("ps", b)], writes=[("vth", blk)])
                if p2 and j == 0:
                    pass
                for t in range(NTT):
                    bf_ = proj(wvf, sf, j, t)
                    t_sg, k_sg = hf()
                    S.op("act", lambda e, o=t_sg, b=bf_: e.activation(o[:], self.bank[b][:], AF.Sigmoid),
                         reads=[("ps", bf_)], writes=[k_sg])
                    t_lf, k_lf = hf()
                    S.op("act", lambda e, o=t_lf, i=t_sg, hd=hd: e.activation(o[:], i[:], AF.Ln, scale=oml[:, hd:hd + 1],
                                                                        bias=lb[:, hd:hd + 1]),
                         reads=[k_sg] + lbk, writes=[k_lf])
                    t_kk, k_kk = hf()
                    S.op("dve", lambda e, o=t_kk, i=t_sg, hd=hd: e.tensor_scalar(o[:], i[:], noml[:, hd:hd + 1], oml[:, hd:hd + 1],
                                                                           ALU.mult, ALU.add),
                         reads=[k_sg] + lbk, writes=[k_kk])
                    t_b, k_b = hf()
                    S.op("dve", lambda e, o=t_b, i=t_lf: e.tensor_tensor_scan(o[:], smask, i[:], 0.0, ALU.mult, ALU.add),
                         reads=[k_lf, "cst2"], writes=[k_b])
                    b3 = t_b[:, :].rearrange("p (c s) -> p c s", s=64)
                    t_d, k_d = hf()
                    S.op("dve", lambda e, o=t_d, b3=b3: e.tensor_tensor(o[:, :].rearrange("p (c s) -> p c s", s=64),
                                                                  b3[:, :, 63:64].to_broadcast([128, 8, 64]), b3, ALU.subtract),
                         reads=[k_b], writes=[k_d])
                    S.op("act", lambda e, o=t_d: e.activation(o[:], o[:], AF.Exp), reads=[k_d], writes=[k_d])
                    t_kl, k_kl = hb()
                    S.op("dve", lambda e, o=t_kl, a=t_kk, b_=t_d: e.tensor_tensor(o[:], a[:], b_[:], ALU.mult),
                         reads=[k_kk, k_d], writes=[k_kl])
                    di = self.rot("dec", 2)
                    dec = self.dec[di]
                    S.op("act", lambda e, dec=dec, b3=b3: e.activation(dec[:, :].rearrange("p (c s) -> p c s", s=1),
                                                                   b3[:, :, 63:64], AF.Exp),
                         reads=[k_b], writes=[("dec", di)])
                    S.op("dve", lambda e, b3=b3, hd=hd, t=t: e.tensor_reduce(
                        self.lsum[:, hd * 2 + t:hd * 2 + t + 1], b3[:, :, 63:64].rearrange("p c s -> p (c s)"),
                        mybir.AxisListType.X, ALU.add), reads=[k_b], writes=[("lsum", hd * 2 + t)])
                    if p2:
                        bq_ = proj(wvq, sq_, j, t)
                        t_qs, k_qs = hf()
                        S.op("act", lambda e, o=t_qs, b=bq_: e.activation(o[:], self.bank[b][:], AF.Silu),
                             reads=[("ps", bq_)], writes=[k_qs])
                        t_e, k_e = hf()
                        S.op("dve", lambda e, o=t_e, b3=b3: e.tensor_tensor(o[:, :].rearrange("p (c s) -> p c s", s=64), b3,
                                                                      b3[:, :, 31:32].to_broadcast([128, 8, 64]), ALU.subtract),
                             reads=[k_b], writes=[k_e])
                        t_e2, k_e2 = hf()
                        S.op("act", lambda e, o=t_e2, i=t_e: e.activation(o[:], i[:], AF.Exp, scale=-1.0),
                             reads=[k_e], writes=[k_e2])
                        S.op("act", lambda e, o=t_e: e.activation(o[:], o[:], AF.Exp), reads=[k_e], writes=[k_e])
                        t_qe, k_qe = hb()
                        S.op("dve", lambda e, o=t_qe, q=t_qs, x_=t_e: e.scalar_tensor_tensor(o[:], q[:], HSCALE, x_[:],
                                                                                       ALU.mult, ALU.mult),
                             reads=[k_qs, k_e], writes=[k_qe])
                        t_ke, k_ke = hb()
                        S.op("dve", lambda e, o=t_ke, a=t_kk, x_=t_e2: e.tensor_tensor(o[:], a[:], x_[:], ALU.mult),
                             reads=[k_kk, k_e2], writes=[k_ke])
                        S.op("act", lambda e, o=t_e2, i=t_b: e.activation(o[:], i[:], AF.Exp), reads=[k_b], writes=[k_e2])
                        t_qb, k_qb = hb()
                        S.op("dve", lambda e, o=t_qb, q=t_qs, x_=t_e2: e.scalar_tensor_tensor(o[:], q[:], HSCALE, x_[:],
                                                                                        ALU.mult, ALU.mult),
                             reads=[k_qs, k_e2], writes=[k_qb])
                    for c in range(8):
                        par = c % 2
                        blk = (t * 8 + c) // 2
                        pr = slice(par * 64, par * 64 + 64)
                        cc = slice(c * 64, c * 64 + 64)
                        if par == 0:
                            ti = self.rot("klT", 2)
                            S.op("pe", lambda e, ti=ti, c=c, kl=t_kl: e.transpose(self.pst[:, ti * 128:(ti + 1) * 128],
                                                                           kl[:, c * 64:c * 64 + 128], self.identb[:]),
                                 reads=[k_kl, "identb"], writes=["pst"])
                            S.op("act", lambda e, ti=ti: e.activation(self.klT[ti][:], self.pst[:, ti * 128:(ti + 1) * 128],
                                                                      AF.Identity),
                                 reads=["pst"], writes=[("klT", ti)])
                        if p2:
                            ai = self.rot("At", 2)
                            S.op("pe", lambda e, ai=ai, pr=pr, cc=cc, ke=t_ke, qe=t_qe: e.matmul(
                                self.bank[3][pr, ai * 64:(ai + 1) * 64], ke[:, cc], qe[:, cc], start=True, stop=True),
                                reads=[k_ke, k_qe], writes=[("ps", 3)])
                            S.op("dve", lambda e, ai=ai, pr=pr: e.tensor_tensor(self.At[ai][pr, :], self.bank[3][pr, ai * 64:(ai + 1) * 64],
                                                                          cm64[pr, :], ALU.mult),
                                 reads=[("ps", 3), "cst2"], writes=[("At", ai)])
                            S.op("pe", lambda e, ai=ai, pr=pr, cc=cc, blk=blk, j=j: e.matmul(
                                self.bank[PSO][:, cc], self.vth[pr, blk, :], self.At[ai][pr, :],
                                start=True, stop=False), reads=[("At", ai), ("vth", blk)], writes=[("ps", PSO)])
                            S.op("pe", lambda e, cc=cc, hd=hd, qb=t_qb: e.matmul(
                                self.bank[PSO][:, cc], self.Sb[:, hd, :], qb[:, cc], start=False, stop=True),
                                reads=[("Sb", hd), k_qb], writes=[("ps", PSO)])
                        ui = self.rot("psU", 2)
                        S.op("pe", lambda e, ui=ui, pr=pr, ti=ti, blk=blk, j=j: e.matmul(
                            self.bank[5][:, 256 + ui * 128:256 + (ui + 1) * 128], self.klT[ti][pr, :],
                            self.vth[pr, blk, :], start=True, stop=True),
                            reads=[("klT", ti), ("vth", blk)], writes=[("ps", 5)])
                        S.op("dve", lambda e, ui=ui, hd=hd, c=c, dec=dec: e.scalar_tensor_tensor(
                            self.Sf[:, hd, :], self.Sf[:, hd, :], dec[:, c:c + 1], self.bank[5][:, 256 + ui * 128:256 + (ui + 1) * 128],
                            ALU.mult, ALU.add), reads=[("Sf", hd), ("dec", di), ("ps", 5)], writes=[("Sf", hd)])
                        if p2:
                            S.op("act", lambda e, hd=hd: e.activation(self.Sb[:, hd, :], self.Sf[:, hd, :], AF.Identity),
                                 reads=[("Sf", hd)], writes=[("Sb", hd)])
                    if p2:
                        bg_ = None
                        sg_, wvg = self._wg
                        bg_ = proj(wvg, sg_, j, t)
                        t_gs, k_gs = hf()
                        S.op("act", lambda e, o=t_gs, b=bg_: e.activation(o[:], self.bank[b][:], AF.Silu),
                             reads=[("ps", bg_)], writes=[k_gs])
                        q = self.rot("sq", 2)
                        S.op("act", lambda e, q=q: e.activation(self.sq[q][:], self.bank[PSO][:], AF.Square),
                             reads=[("ps", PSO)], writes=[("sq", q)])
                        S.op("pe", lambda e, q=q: e.matmul(self.bank[6][:], self.ones_bf[:], self.sq[q][:], start=True, stop=True),
                             reads=[("sq", q), "ones_bf"], writes=[("ps", 6)])
                        r = self.rot("rstd", 2)
                        S.op("act", lambda e, r=r: e.activation(self.rstd[r][:], self.bank[6][:], AF.Sqrt,
                                                              bias=self.epsv[:, 0:1], scale=1.0 / 128),
                             reads=[("ps", 6), "epsv"], writes=[("rstd", r)])
                        S.op("dve", lambda e, r=r: e.reciprocal(self.rstd[r][:], self.rstd[r][:]),
                             reads=[("rstd", r)], writes=[("rstd", r)])
                        t_o, k_o = hf()
                        S.op("dve", lambda e, o=t_o, r=r, hd=hd: e.scalar_tensor_tensor(
                            o[:], self.bank[PSO][:], self.vec[:, C_OG + hd:C_OG + hd + 1], self.rstd[r][:], ALU.mult, ALU.mult),
                            reads=[("ps", PSO), ("rstd", r), "vec"], writes=[k_o])
                        S.op("dve", lambda e, o=t_o, g_=t_gs, j=j, t=t, mid=mid: e.tensor_tensor(
                            mid[:, j, t * TW:(t + 1) * TW], o[:], g_[:], ALU.mult),
                            reads=[k_o, k_gs], writes=[("mid", midi, j, t)])
            if p2:
                self.stage2(l, 32, self.hgrn_w_out[hg * 512:(hg + 1) * 512, :], midi)
        if not p2:
            lt = self.lsum[:, :].rearrange("p (h t) -> p h t", t=2)
            S.op("dve", lambda e: e.tensor_tensor(self.lbv[:, 0:16].rearrange("p (h t) -> p h t", t=1), lt[:, :, 0:1], lt[:, :, 1:2], ALU.add),
                 reads=[("lsum", i) for i in range(32)] + lbk, writes=["lbv0"])
            S.op("act", lambda e: e.activation(self.lbv[:, 0:16], self.lbv[:, 0:16], AF.Exp), reads=["lbv0"], writes=["lbv0"])
            S.dma("sp", self.dt_d, self.lbv[:, 0:16], sem="st", reads=["lbv0"])
            S.dma("sp", self.U_d, self.Sf[:, :, :].rearrange("p h v -> p (h v)"), sem="st", reads=[("Sf", hd) for hd in range(16)])

    def mlp(self, l):
        S = self.S
        self.prenorm(l, 1)
        for fg in range(DFF // 512):
            s, wv = self.wload("kn", self.mlp_w1[l, :, fg * 512:(fg + 1) * 512])
            midi = self.rot("mid", 2)
            mid = self.mid[midi]
            for j in range(4):
                for t in range(NTT):
                    b = self.rot("projps", 4)
                    for k in range(KC):
                        S.op("pe", lambda e, j=j, t=t, k=k, b=b, wv=wv: e.matmul(
                            self.bank[b][:], wv[:, k, j * 128:(j + 1) * 128], self.h[:, k, t * TW:(t + 1) * TW],
                            start=(k == 0), stop=(k == KC - 1)),
                            reads=self.wkeys(s) + [("h", k, t)], writes=[("ps", b)])
                    f = self.rot("hf", self.NHF)
                    S.op("act", lambda e, b=b, f=f: e.activation(self.hf[f][:], self.bank[b][:], AF.Relu),
                         reads=[("ps", b)], writes=[("hf", f)])
                    S.op("pool", lambda e, j=j, t=t, f=f, mid=mid: e.tensor_tensor(
                        mid[:, j, t * TW:(t + 1) * TW], self.hf[f][:], self.hf[f][:], ALU.mult),
                        reads=[("hf", f)], writes=[("mid", midi, j, t)])
            self.stage2(l, 80, self.mlp_w2[l, fg * 512:(fg + 1) * 512, :], midi)

    def epilogue(self):
        S = self.S
        os_ = self.out_d.rearrange("(k p) t -> p k t", p=128)
        for c in range(KC):
            S.dma("sp", os_[:, c, :], self.x[:, c, :], sem="st", reads=[("x", c, 0), ("x", c, 1)])
        S.finish(final_waits=["st"])


    def build(self):
        self.prologue()
        for l in self.layers:
            self.mod_layer(l)
        for l in self.layers:
            if l == 0:
                self.attention()
            else:
                self.hgrn()
                if self.hpass == 1:
                    self.S.finish(final_waits=["st"])
                    return self.nc
            if self.debug != "nomlp":
                self.mlp(l)
        self.epilogue()
        return self.nc


def host_inputs(inputs, layers=(0, 1), x_override=None, debug=None, hpass=2, Uall=None, dall=None):
    f32 = np.float32
    x = np.asarray(inputs["x"], f32)[0] if x_override is None else x_override
    xT = np.ascontiguousarray(x.T)
    s_idx = np.arange(128)[:, None]
    t_idx = np.arange(256)[None, :]
    dist = t_idx - s_idx
    dmat = np.where((dist >= 0) & (dist < 128), -dist, NEG).astype(f32)
    cst2 = np.zeros((128, 64 + 128 + 512), f32)
    cst2[:, 0:64] = ((np.arange(128)[:, None] % 64) <= np.arange(64)[None, :])
    cst2[:, 64:192] = np.eye(128)
    cst2[:, 192:704] = 1.0
    cst2[:, 192:704:64] = 0.0
    base = np.zeros((128, NV), f32)
    base[:, C_C:C_C + 16] = _fm(inputs["c"][0])
    for l in range(2):
        base[:, C_L[l]:C_L[l] + 16] = _fm(inputs["norm_mix"][l])
        base[:, C_L[l] + 16:C_L[l] + 32] = _fm(inputs["norm_mlp"][l])
        base[:, C_L[l] + 32:C_L[l] + 128] = _fm(inputs["mod_b"][l])
        base[:, C_LB + 16 * l:C_LB + 16 * (l + 1)] = _fm(inputs["hgrn_lb_logits"][l])
    base[:, C_OG:C_OG + 16] = np.asarray(inputs["hgrn_o_gain"][0], f32).T
    base[:, C_QG] = np.tile(np.asarray(inputs["attn_q_gain"][0], f32), 2)
    base[:, C_KG] = np.tile(np.asarray(inputs["attn_k_gain"][0], f32), 2)
    sk = np.asarray(inputs["attn_sinks"][0], f32)
    base[:, C_SINK:C_SINK + 16] = sk[(2 * np.arange(16)[None, :] + (np.arange(128)[:, None] // 64))]
    maps = []
    for i in range(NCORES):
        v = base.copy()
        v[:, C_HB] = NEG if i == 0 else 0.0
        for j in range(NCORES):
            v[:, C_MS + j] = 1.0 if j < i else 0.0
            v[:, C_OMS + j] = 0.0 if j < i else 1.0
        m = {"xT": np.ascontiguousarray(xT[:, i * NT:(i + 1) * NT]), "vecs": v,
             "mod_w": np.ascontiguousarray(np.asarray(inputs["mod_w"], f32)[list(layers)])}
        if debug != "nomlp" and not (1 in layers and hpass == 1):
            m["mlp_w1"] = np.asarray(inputs["mlp_w1"], f32)
            m["mlp_w2"] = np.asarray(inputs["mlp_w2"], f32)
        if 0 in layers:
            m["xh"] = np.zeros((D, 128), f32) if i == 0 else np.ascontiguousarray(xT[:, i * NT - 128:i * NT])
            m["dmat"] = dmat
            m["attn_w_in"] = np.asarray(inputs["attn_w_in"][0], f32)
            m["attn_w_out"] = np.asarray(inputs["attn_w_out"][0], f32)
        if 1 in layers:
            m["cst2"] = cst2
            m["hgrn_w_in"] = np.asarray(inputs["hgrn_w_in"][0], f32)
            if hpass == 2:
                m["hgrn_w_out"] = np.asarray(inputs["hgrn_w_out"][0], f32)
                m["Uall"] = Uall
                m["dall"] = dall
        maps.append(m)
    return maps


def run_layers(inputs, layers=(0, 1), x_override=None, trace=False, debug=None, hpass=2, Uall=None, dall=None):
    nc = Builder(layers=layers, debug=debug, hpass=hpass).build()
    maps = host_inputs(inputs, layers, x_override, debug, hpass, Uall, dall)
    res = run_bass_kernel_spmd(nc, maps, core_ids=list(range(NCORES)), trace=trace)
    if 1 in layers and hpass == 1:
        U = np.ascontiguousarray(np.stack([r["U_out"] for r in res.results], axis=0))
        dt = np.ascontiguousarray(np.concatenate([r["dt_out"] for r in res.results], axis=1))
        return (U, dt), res
    outT = np.concatenate([r["outT"] for r in res.results], axis=1)
    return np.ascontiguousarray(outT.T)[None].astype(np.float32), res


def kernel(**inputs):
    x1, _ = run_layers(inputs, layers=(0,))
    (U, dt), _ = run_layers(inputs, layers=(1,), x_override=x1[0], hpass=1)
    out, _ = run_layers(inputs, layers=(1,), x_override=x1[0], hpass=2, Uall=U, dall=dt)
    return out
```

```python
import numpy as np
from contextlib import ExitStack
import concourse.bass as bass
import concourse.mybir as mybir
from concourse.bass_utils import run_bass_kernel_spmd

F32 = mybir.dt.float32
BF16 = mybir.dt.bfloat16
AF = mybir.ActivationFunctionType
ALU = mybir.AluOpType


class _Op:
    __slots__ = ("eng", "fn", "kind", "sem", "pos", "waits", "signal", "val", "idx", "inc")


class Sched:
    ENGS = ("pe", "act", "dve", "pool", "sp")

    def __init__(self, nc):
        self.nc = nc
        self.stack = ExitStack()
        self.ops = {e: [] for e in self.ENGS}
        self.last_writer = {}
        self.readers = {}
        self.waited = {e: {} for e in self.ENGS}
        self.dma_count = {}
        self.total_sems = set()
        self.n = 0

    def sbuf(self, name, shape, dtype):
        return self.stack.enter_context(self.nc.sbuf_tensor("sb_" + name, list(shape), dtype))

    def psum(self, name, shape, dtype=F32):
        return self.stack.enter_context(self.nc.psum_tensor("ps_" + name, list(shape), dtype))

    def _deps(self, op, reads, writes):
        deps = []
        for k in reads:
            w = self.last_writer.get(k)
            if w is not None:
                deps.append(w)
        for k in writes:
            rd = self.readers.get(k, ())
            if rd:
                deps.extend(rd)
            else:
                w = self.last_writer.get(k)
                if w is not None:
                    deps.append(w)
        waits = {}
        for d in deps:
            if d is op:
                continue
            if d.kind == "dma":
                key = ("dma", d.sem)
                v = d.val
            else:
                if d.eng == op.eng and op.kind != "dma" and d.eng == "pe":
                    continue
                key = ("eng", d.eng)
                v = d.pos
            if self.waited[op.eng].get(key, -1) >= v:
                continue
            if waits.get(key, (-1, None))[0] < v:
                waits[key] = (v, d)
        for key, (v, d) in waits.items():
            self.waited[op.eng][key] = v
            d.signal = True
        op.waits = list(waits.items())
        for k in writes:
            self.last_writer[k] = op
            self.readers[k] = []
        for k in reads:
            self.readers.setdefault(k, []).append(op)

    def op(self, eng, fn, reads=(), writes=()):
        o = _Op()
        o.eng, o.fn, o.kind, o.sem = eng, fn, "cmp", None
        o.pos = len(self.ops[eng])
        o.signal = False
        o.val = None
        o.idx = self.n
        self.n += 1
        self._deps(o, list(reads), list(writes))
        self.ops[eng].append(o)
        return o

    def dma(self, eng, out, in_, sem, reads=(), writes=(), inc=16, **kw):
        o = _Op()
        o.eng, o.kind, o.sem = eng, "dma", sem
        o.fn = lambda e: e.dma_start(out=out, in_=in_, **kw)
        o.pos = len(self.ops[eng])
        o.signal = True
        o.inc = inc
        self.dma_count[sem] = self.dma_count.get(sem, 0) + inc
        o.val = self.dma_count[sem]
        o.idx = self.n
        self.n += 1
        self._deps(o, list(reads), list(writes))
        self.ops[eng].append(o)
        return o

    def finish(self, final_waits=()):
        nc = self.nc
        st = self.stack
        esem = {e: st.enter_context(nc.semaphore("sem_" + e)) for e in self.ENGS}
        dsem = {s: st.enter_context(nc.semaphore("dsem_" + s)) for s in self.dma_count}
        for e in self.ENGS:
            c = 0
            for o in self.ops[e]:
                if o.kind == "cmp" and o.signal:
                    c += 1
                    o.val = c
        block = st.enter_context(nc.Block())
        engobj = {"pe": "tensor", "act": "scalar", "dve": "vector", "pool": "gpsimd", "sp": "sync"}

        def emit(ename):
            def body(eng):
                for o in self.ops[ename]:
                    for key, (v, d) in o.waits:
                        if key[0] == "dma":
                            if key[1] in self.total_sems:
                                v = self.dma_count[key[1]]
                            eng.wait_ge(dsem[key[1]], v)
                        else:
                            eng.wait_ge(esem[key[1]], d.val)
                    ins = o.fn(eng)
                    if o.kind == "dma":
                        if o.inc == 16:
                            ins.then_inc(dsem[o.sem], 16)
                        else:
                            ins.then_inc(dsem[o.sem])
                    elif o.signal:
                        ins.then_inc(esem[ename], 1)
                if ename == "sp":
                    for s in final_waits:
                        eng.wait_ge(dsem[s], self.dma_count[s])
            return body

        for ename in self.ENGS:
            getattr(block, engobj[ename])(emit(ename))
        st.close()


NCORES = 8
SEQ = 8192
D = 2048
NT = SEQ // NCORES
KC = D // 128
TW = 512
NTT = NT // TW
DFF = 4 * D
EPS = 1e-6
NEG = -30000.0
NSLOT = 2

C_C = 0
C_L = [16, 16 + 128]
C_LB = 16 + 256
C_OG = C_LB + 32
C_QG = C_OG + 16
C_KG = C_QG + 1
C_SINK = C_KG + 1
C_HB = C_SINK + 16
C_MS = C_HB + 1
C_OMS = C_MS + 8
NV = C_OMS + 8
HSCALE = 1.0 / float(np.sqrt(128.0))


def _fm(v):
    return np.ascontiguousarray(np.asarray(v, np.float32).reshape(-1, 128).T)


class Builder:
    def __init__(self, layers=(0, 1), debug=None, hpass=2):
        self.layers = layers
        self.debug = debug
        self.hpass = hpass
        nc = self.nc = bass.Bass("TRN2", target_bir_lowering=False)
        self.S = S = Sched(nc)
        dr = lambda n, s, k="ExternalInput": nc.dram_tensor(n, list(s), F32, kind=k).ap()
        self.xT_d = dr("xT", [D, NT])
        if 0 in layers:
            self.xh_d = dr("xh", [D, 128])
        self.vec_d = dr("vecs", [128, NV])
        if 0 in layers:
            self.dmat_d = dr("dmat", [128, 256])
        if 1 in layers:
            self.cst2_d = dr("cst2", [128, 64 + 128 + 512])
        self.fused = (tuple(layers) == (0, 1) and hpass == 0)
        self.mod_w = dr("mod_w", [len(layers), D, 6 * D])
        if debug != "nomlp" and not (1 in layers and hpass == 1):
            self.mlp_w1 = dr("mlp_w1", [2, D, DFF])
            self.mlp_w2 = dr("mlp_w2", [2, DFF, D])
        if 0 in layers:
            self.attn_w_in = dr("attn_w_in", [D, 2560])
            self.attn_w_out = dr("attn_w_out", [D, D])
        if 1 in layers:
            self.hgrn_w_in = dr("hgrn_w_in", [D, 4 * D])
            if hpass != 1:
                self.hgrn_w_out = dr("hgrn_w_out", [D, D])
        if 1 in layers and hpass == 1:
            self.U_d = dr("U_out", [128, 2048], "ExternalOutput")
            self.dt_d = dr("dt_out", [128, 16], "ExternalOutput")
        else:
            self.out_d = dr("outT", [D, NT], "ExternalOutput")
        if 1 in layers and hpass == 2:
            self.Uall_d = dr("Uall", [NCORES, 128, 2048])
            self.dall_d = dr("dall", [128, NCORES * 16])

        self.x = S.sbuf("x", [128, KC, NT], F32)
        self.h = S.sbuf("h", [128, KC, NT], BF16)
        self.wslot = [S.sbuf("w%d" % i, [128, 8192], BF16) for i in range(NSLOT)]
        self.mid = [S.sbuf("mid%d" % i, [128, 4, NT], BF16) for i in range(2)]
        self.vec = S.sbuf("vec", [128, NV], F32)
        self.modT = S.sbuf("modT", [128, 2 * 96], F32)
        self.coef = S.sbuf("coef", [128, 2 * 96], F32)
        self.ones_bf = S.sbuf("ones_bf", [128, 128], BF16)
        self.blk_bf = S.sbuf("blk_bf", [128, 128], BF16)
        self.one_f = S.sbuf("one_f", [1, 1], F32)
        self.cond = S.sbuf("cond", [128, KC], BF16)
        self.sq = [S.sbuf("sq%d" % i, [128, TW], BF16) for i in range(2)]
        self.rstd = [S.sbuf("rstd%d" % i, [128, TW], F32) for i in range(2)]
        self.epsv = S.sbuf("epsv", [128, 1], F32)
        self.bar = S.sbuf("bar", [128, 1], F32)
        self.NHF = 10 if 1 in layers else 4
        self.hf = [S.sbuf("hf%d" % i, [128, TW], F32) for i in range(self.NHF)]
        self.bank = [S.psum("bank%d" % i, [128, TW], F32) for i in range(8)]
        self.pst = self.bank[7][:, :].bitcast(BF16)
        self.MODB, self.MODT = (4, 5), 7
        REG = 32768
        self.region = S.sbuf("region", [128, REG // 4], F32)
        state = {"off": 0}

        def carve(shape, dtype):
            n = 1
            for d_ in shape[1:]:
                n *= d_
            nbytes = n * (2 if dtype == BF16 else 4)
            nbytes = (nbytes + 31) // 32 * 32
            a = self.region[:, state["off"] // 4:(state["off"] + nbytes) // 4]
            state["off"] += nbytes
            assert state["off"] <= REG, state["off"]
            if dtype == BF16:
                a = a.bitcast(BF16)
            a = a[:, 0:n]
            if len(shape) == 3:
                a = a.rearrange("p (a b) -> p a b", b=shape[2])
            return a

        if 0 in layers:
            state["off"] = 0
            self.xh = carve([128, KC, 128], F32)
            self.hh = carve([128, KC, 128], BF16)
            self.dmat = carve([128, 256], F32)
            self.esink = carve([128, KC], F32)
            self.qg8 = carve([128, 1], F32)
            self.kT = carve([128, 4, NT + 128], BF16)
            self.vtm = carve([128, 9, 256], BF16)
            self.pT = [carve([128, 256], BF16) for i in range(4)]
            self.tmpa = [carve([128, 256], F32) for i in range(3)]
            self.attn_keys = ([("xh", c, 0) for c in range(KC)] + [("hh", c, 0) for c in range(KC)] + ["dmat", "esink", "qg8"]
                              + [("kT", j, ko) for j in range(4) for ko in (0, 128, 128 + TW)]
                              + [("vtm", b) for b in range(9)] + [("pT", i) for i in range(4)] + [("tmpa", i) for i in range(3)])
        if 1 in layers:
            state["off"] = 0
            self.cst2 = carve([128, 64 + 128 + 512], F32)
            self.identb = carve([128, 128], BF16)
            self.vth = carve([128, 8, 128], BF16)
            self.Sf = carve([128, 16, 128], F32)
            self.Sb8 = carve([128, 8, 128], BF16)
            self.Stmp = carve([128, 128], F32)
            self.klT4 = carve([128, 4, 128], BF16)
            self.At8 = carve([128, 8, 64], BF16)
            self.hb = [carve([128, TW], BF16) for i in range(6)]
            self.dec = [carve([128, 8], F32) for i in range(2)]
            self.lsum = carve([128, 32], F32)
            self.lbv = carve([128, 48], F32)
            self.dtot = carve([128, 16], F32)
            self.dall = carve([128, NCORES * 16], F32)
            self.avec = carve([128, 16], F32)
            self.klT8 = carve([128, 8, 128], BF16)
            self.hgrn_keys = (["cst2", "identb", "lbv0", "lbv1", "lbv2", "dall", "avec", "dtot"]
                              + [("vth", b) for b in range(8)] + [("Sf", h_) for h_ in range(16)] + [("Sb8", c_) for c_ in range(8)]
                              + ["Stmp", "klT4"] + [("At8", c_) for c_ in range(8)] + [("hb", i) for i in range(6)]
                              + [("dec", i) for i in range(2)] + [("lsum", i) for i in range(32)] + ["klT8"])
        self.slot_i = 0
        self.cnt = {}

    def rot(self, name, n):
        i = self.cnt.get(name, 0)
        self.cnt[name] = i + 1
        return i % n

    def wload(self, kind, src, ncols=512):
        S = self.S
        s = self.slot_i % NSLOT
        self.slot_i += 1
        slot = self.wslot[s]
        if kind == "kn":
            v = slot[:, 0:16 * ncols].rearrange("p (k n) -> p k n", n=ncols)
            sv = src.rearrange("(k p) n -> p k n", p=128)
            for part in range(4):
                S.dma("pool", v[:, part * 4:(part + 1) * 4, :], sv[:, part * 4:(part + 1) * 4, :],
                      sem="w%d" % s, writes=[("w", s, 2 * part), ("w", s, 2 * part + 1)])
        else:
            v = slot[:, :].rearrange("p (k n) -> p k n", n=2048)
            sv = src.rearrange("(k p) n -> p k n", p=128)
            for part in range(4):
                S.dma("pool", v[:, part:part + 1, :], sv[:, part:part + 1, :],
                      sem="w%d" % s, writes=[("w", s, 2 * part), ("w", s, 2 * part + 1)])
        return s, v

    def wkeys(self, s):
        return [("w", s, p) for p in range(8)]

    def prologue(self):
        S = self.S
        S.total_sems.add("cst")
        S.total_sems.add("xin")
        S.dma("sp", self.vec[:], self.vec_d, sem="cst", writes=["vec"])
        xs = self.xT_d.rearrange("(k p) t -> p k t", p=128)
        for c in range(KC):
            S.dma("sp", self.x[:, c, :], xs[:, c, :], sem="xin", writes=[("x", c, 0), ("x", c, 1)])
        S.op("dve", lambda e: e.memset(self.epsv[:], EPS), writes=["epsv"])
        if 0 in self.layers:
            xhs = self.xh_d.rearrange("(k p) t -> p k t", p=128)
            S.dma("sp", self.xh[:], xhs, sem="xin", writes=[("xh", c, 0) for c in range(KC)])
            S.dma("sp", self.dmat[:], self.dmat_d, sem="cst", writes=["dmat"])
            S.op("act", lambda e: e.activation(self.esink[:], self.vec[:, C_SINK:C_SINK + 16], AF.Exp),
                 reads=["vec"], writes=["esink"])
            S.op("dve", lambda e: e.tensor_scalar(self.qg8[:], self.vec[:, C_QG:C_QG + 1], 0.125, None, ALU.mult),
                 reads=["vec"], writes=["qg8"])
        S.op("dve", lambda e: e.memset(self.ones_bf[:], 1.0), writes=["ones_bf"])
        S.op("dve", lambda e: e.memset(self.blk_bf[:], 0.0), writes=["blk_bf"])
        S.op("dve", lambda e: e.memset(self.blk_bf[0:64, 0:64], 1.0), reads=["blk_bf"], writes=["blk_bf"])
        S.op("dve", lambda e: e.memset(self.blk_bf[64:128, 64:128], 1.0), reads=["blk_bf"], writes=["blk_bf"])
        S.op("dve", lambda e: e.memset(self.one_f[:], 1.0), writes=["one_f"])
        S.op("act", lambda e: e.activation(self.cond[:], self.vec[:, C_C:C_C + 16], AF.Silu),
             reads=["vec"], writes=["cond"])

    def mod_blocks(self, l, blocks):
        S = self.S
        for nb in blocks:
            s, wv = self.wload("kn", self.mod_w[self.layers.index(l), :, nb * 512:(nb + 1) * 512])
            b = self.rot("projps", 4)
            for k in range(KC):
                S.op("pe", lambda e, k=k, b=b, wv=wv: e.matmul(self.bank[b][0:1, :], self.cond[:, k:k + 1], wv[:, k, :],
                                                         start=(k == 0), stop=(k == KC - 1)),
                     reads=["cond"] + self.wkeys(s), writes=[("ps", b)])
            r = self.rot("hf", self.NHF)
            S.op("act", lambda e, b=b, r=r: e.activation(self.hf[r][0:1, :], self.bank[b][0:1, :], AF.Identity),
                 reads=[("ps", b)], writes=[("hf", r)])
            b2 = self.rot("projps", 4)
            for j in range(4):
                S.op("pe", lambda e, r=r, j=j, b2=b2: e.matmul(self.bank[b2][:, j:j + 1], self.hf[r][0:1, j * 128:(j + 1) * 128],
                                                              self.one_f[0:1, 0:1], start=True, stop=True),
                     reads=[("hf", r), "one_f"], writes=[("ps", b2)])
            c0 = l * 96 + nb * 4
            vb = C_L[l] + 32 + nb * 4
            S.op("dve", lambda e, b2=b2, c0=c0, vb=vb: e.tensor_tensor(self.modT[:, c0:c0 + 4], self.bank[b2][:, 0:4],
                                                                     self.vec[:, vb:vb + 4], ALU.add),
                 reads=[("ps", b2), "vec"], writes=[("modTb", l, nb)])

    def mod_coef_part(self, l, part):
        S = self.S
        mT = self.modT[:, l * 96:(l + 1) * 96]
        cf = self.coef[:, l * 96:(l + 1) * 96]
        mk = lambda lo, hi: [("modTb", l, nb) for nb in range(lo, hi)]
        if part in ("a1b1", "a2b2"):
            dst, sh, sc, nrm, lo = (0, 0, 16, C_L[l], 0) if part == "a1b1" else (48, 48, 64, C_L[l] + 16, 12)
            S.op("dve", lambda e: e.scalar_tensor_tensor(cf[:, dst:dst + 16], mT[:, sc:sc + 16], 1.0, self.vec[:, nrm:nrm + 16],
                                                         ALU.add, ALU.mult),
                 reads=mk(lo, lo + 8) + ["vec"], writes=[("coef", l, dst)])
            S.op("dve", lambda e: e.tensor_copy(cf[:, dst + 16:dst + 32], mT[:, sh:sh + 16]),
                 reads=mk(lo, lo + 8), writes=[("coef", l, dst + 16)])
        else:
            off, lo = (32, 8) if part == "g1" else (80, 20)
            S.op("dve", lambda e: e.tensor_copy(cf[:, off:off + 16], mT[:, off:off + 16]),
                 reads=mk(lo, lo + 4), writes=[("coef", l, off)])

    def mod_shard(self):
        S = self.S
        nc = self.nc
        pT = self.MODT
        for nb in range(6):
            s, wv = self.wload("kn", self.mod_ws[:, nb * 512:(nb + 1) * 512])
            b = self.MODB[self.rot("modps", 2)]
            for k in range(KC):
                S.op("pe", lambda e, k=k, b=b, wv=wv: e.matmul(self.bank[b][0:1, :], self.cond[:, k:k + 1], wv[:, k, :],
                                                         start=(k == 0), stop=(k == KC - 1)),
                     reads=["cond"] + self.wkeys(s), writes=[("ps", b)])
            r = self.rot("hf", self.NHF)
            S.op("act", lambda e, b=b, r=r: e.activation(self.hf[r][0:1, :], self.bank[b][0:1, :], AF.Identity),
                 reads=[("ps", b)], writes=[("hf", r)])
            for j in range(4):
                col = nb * 4 + j
                S.op("pe", lambda e, r=r, j=j, col=col: e.matmul(self.bank[pT][:, col:col + 1],
                                                                self.hf[r][0:1, j * 128:(j + 1) * 128],
                                                                self.one_f[0:1, 0:1], start=True, stop=True),
                     reads=[("hf", r), "one_f"], writes=[("ps", pT)])
        S.op("act", lambda e: e.activation(self.coef[:, 0:24], self.bank[pT][:, 0:24], AF.Identity),
             reads=[("ps", pT)], writes=["modloc"])
        bounce = nc.dram_tensor("mod_bounce", [128, 24], F32)
        gath = nc.dram_tensor("mod_gath", [NCORES * 128, 24], F32, addr_space="Shared")
        S.dma("sp", bounce[:, :], self.coef[:, 0:24], sem="mbnc", reads=["modloc"], writes=["mod_bounce"])
        o = S.dma("pool", None, None, sem="ccm", reads=["mod_bounce"], writes=["mod_gath"], inc=1)
        o.fn = lambda e: e.collective_compute("AllGather", ALU.bypass, replica_groups=[list(range(NCORES))],
                                              ins=[bounce.ap().opt()], outs=[gath.ap().opt()])
        S.dma("sp", self.modT[:, :].rearrange("p (r c) -> p r c", c=24), gath.ap().rearrange("(r p) c -> p r c", p=128),
              sem="mgl", reads=["mod_gath"], writes=[("modTraw", 0), ("modTraw", 1)])
        for l in range(2):
            mT = self.modT[:, l * 96:(l + 1) * 96]
            S.op("dve", lambda e, mT=mT, l=l: e.tensor_tensor(mT, mT, self.vec[:, C_L[l] + 32:C_L[l] + 128], ALU.add),
                 reads=[("modTraw", l), "vec", "modloc"], writes=[("modT", l)])
            self.mod_coefs(l)

    def mod_coefs(self, l):
        S = self.S
        mT = self.modT[:, l * 96:(l + 1) * 96]
        cf = self.coef[:, l * 96:(l + 1) * 96]
        for (dst, src_scale, nrm) in ((0, 16, C_L[l]), (48, 64, C_L[l] + 16)):
            S.op("dve", lambda e, dst=dst, src_scale=src_scale, nrm=nrm: e.scalar_tensor_tensor(
                cf[:, dst:dst + 16], mT[:, src_scale:src_scale + 16], 1.0, self.vec[:, nrm:nrm + 16], ALU.add, ALU.mult),
                reads=[("modT", l), "vec", "modloc"], writes=[("coef", l, dst)])
        for (dst, src) in ((16, 0), (32, 32), (64, 48), (80, 80)):
            S.op("dve", lambda e, dst=dst, src=src: e.tensor_copy(cf[:, dst:dst + 16], mT[:, src:src + 16]),
                 reads=[("modT", l), "modloc"], writes=[("coef", l, dst)])

    def mod_layer(self, l):
        S = self.S
        pT = self.MODT
        for nb in range(24):
            s, wv = self.wload("kn", self.mod_w[self.layers.index(l), :, nb * 512:(nb + 1) * 512])
            b = self.MODB[self.rot("modps", 2)]
            for k in range(KC):
                S.op("pe", lambda e, k=k, b=b, wv=wv: e.matmul(self.bank[b][0:1, :], self.cond[:, k:k + 1], wv[:, k, :],
                                                         start=(k == 0), stop=(k == KC - 1)),
                     reads=["cond"] + self.wkeys(s), writes=[("ps", b)])
            r = self.rot("hf", self.NHF)
            S.op("act", lambda e, b=b, r=r: e.activation(self.hf[r][0:1, :], self.bank[b][0:1, :], AF.Identity),
                 reads=[("ps", b)], writes=[("hf", r)])
            for j in range(4):
                col = nb * 4 + j
                S.op("pe", lambda e, r=r, j=j, col=col: e.matmul(self.bank[pT][:, col:col + 1],
                                                                self.hf[r][0:1, j * 128:(j + 1) * 128],
                                                                self.one_f[0:1, 0:1], start=True, stop=True),
                     reads=[("hf", r), "one_f"], writes=[("ps", pT)])
        mT = self.modT[:, l * 96:(l + 1) * 96]
        S.op("dve", lambda e: e.tensor_tensor(mT, self.bank[pT][:, 0:96], self.vec[:, C_L[l] + 32:C_L[l] + 128], ALU.add),
             reads=[("ps", pT), "vec"], writes=[("modT", l)])
        self.mod_coefs(l)

    def coefk(self, l, offs=(0, 16, 32, 48, 64, 80)):
        return [("coef", l, d) for d in offs]

    def prenorm(self, l, which, src=None, dst=None, ntok=NT, keyx="x", keyh="h"):
        S = self.S
        src = self.x if src is None else src
        dst = self.h if dst is None else dst
        cf = self.coef[:, l * 96:(l + 1) * 96]
        ao, bo = (0, 16) if which == 0 else (48, 64)
        ntt = (ntok + TW - 1) // TW
        for t in range(ntt):
            w = min(TW, ntok - t * TW)
            tok = slice(t * TW, t * TW + w)
            pb = 6
            for c in range(KC):
                q = self.rot("sq", 2)
                S.op("act", lambda e, c=c, q=q, tok=tok, w=w: e.activation(self.sq[q][:, 0:w], src[:, c, tok], AF.Square),
                     reads=[(keyx, c, t)], writes=[("sq", q)])
                S.op("pe", lambda e, c=c, q=q, w=w: e.matmul(self.bank[pb][:, 0:w], self.ones_bf[:], self.sq[q][:, 0:w],
                                                        start=(c == 0), stop=(c == KC - 1)),
                     reads=[("sq", q), "ones_bf"], writes=[("ps", pb)])
            r = self.rot("rstd", 2)
            S.op("act", lambda e, r=r, w=w: e.activation(self.rstd[r][:, 0:w], self.bank[pb][:, 0:w], AF.Sqrt,
                                                    bias=self.epsv[:, 0:1], scale=1.0 / D),
                 reads=[("ps", pb), "epsv"], writes=[("rstd", r)])
            S.op("dve", lambda e, r=r, w=w: e.reciprocal(self.rstd[r][:, 0:w], self.rstd[r][:, 0:w]),
                 reads=[("rstd", r)], writes=[("rstd", r)])
            for c in range(KC):
                f = self.rot("hf", self.NHF)
                S.op("dve", lambda e, c=c, f=f, r=r, tok=tok, w=w: e.scalar_tensor_tensor(
                    self.hf[f][:, 0:w], src[:, c, tok], cf[:, ao + c:ao + c + 1], self.rstd[r][:, 0:w], ALU.mult, ALU.mult),
                    reads=[(keyx, c, t), ("rstd", r)] + self.coefk(l, (ao,)), writes=[("hf", f)])
                S.op("act", lambda e, c=c, f=f, tok=tok, w=w: e.activation(dst[:, c, tok], self.hf[f][:, 0:w], AF.Identity,
                                                                   bias=cf[:, bo + c:bo + c + 1]),
                     reads=[("hf", f)] + self.coefk(l, (bo,)), writes=[(keyh, c, t)])

    def stage2(self, l, gate_off, src_rows, midi):
        S = self.S
        s, wv = self.wload("rows", src_rows)
        cf = self.coef[:, l * 96:(l + 1) * 96]
        mid = self.mid[midi]
        for n in range(KC):
            for t in range(NTT):
                b = self.rot("projps", 4)
                for j in range(4):
                    S.op("pe", lambda e, n=n, t=t, j=j, b=b, wv=wv: e.matmul(
                        self.bank[b][:], wv[:, j, n * 128:(n + 1) * 128], mid[:, j, t * TW:(t + 1) * TW],
                        start=(j == 0), stop=(j == 3)),
                        reads=self.wkeys(s) + [("mid", midi, j, t)], writes=[("ps", b)])
                S.op("dve", lambda e, n=n, t=t, b=b: e.scalar_tensor_tensor(
                    self.x[:, n, t * TW:(t + 1) * TW], self.bank[b][:], cf[:, gate_off + n:gate_off + n + 1],
                    self.x[:, n, t * TW:(t + 1) * TW], ALU.mult, ALU.add),
                    reads=[("ps", b), ("x", n, t)] + self.coefk(l, (gate_off,)), writes=[("x", n, t)])

    def qknorm(self, b, w, gain_ap, gain_key, out_ap, out_key):
        S = self.S
        pb = 6
        q = self.rot("sq", 2)
        S.op("act", lambda e: e.activation(self.sq[q][:, 0:w], self.bank[b][:, 0:w], AF.Square),
             reads=[("ps", b)], writes=[("sq", q)])
        S.op("pe", lambda e: e.matmul(self.bank[pb][:, 0:w], self.blk_bf[:], self.sq[q][:, 0:w], start=True, stop=True),
             reads=[("sq", q), "blk_bf"], writes=[("ps", pb)])
        r = self.rot("rstd", 2)
        S.op("act", lambda e: e.activation(self.rstd[r][:, 0:w], self.bank[pb][:, 0:w], AF.Sqrt,
                                           bias=self.epsv[:, 0:1], scale=1.0 / 64),
             reads=[("ps", pb), "epsv"], writes=[("rstd", r)])
        S.op("dve", lambda e: e.reciprocal(self.rstd[r][:, 0:w], self.rstd[r][:, 0:w]),
             reads=[("rstd", r)], writes=[("rstd", r)])
        S.op("dve", lambda e: e.scalar_tensor_tensor(out_ap, self.bank[b][:, 0:w], gain_ap, self.rstd[r][:, 0:w],
                                                     ALU.mult, ALU.mult),
             reads=[("ps", b), ("rstd", r), gain_key], writes=[out_key])

    def attention(self, do_prenorm=True):
        S = self.S
        l = 0
        if do_prenorm:
            self.prenorm(l, 0)
            self.prenorm(l, 0, src=self.xh, dst=self.hh, ntok=128, keyx="xh", keyh="hh")
        hsrc = [(self.hh, "hh", 0, 0, 128)] + [(self.h, "h", t, t * TW, TW) for t in range(NTT)]
        s = self.slot_i % NSLOT
        self.slot_i += 1
        wv = self.wslot[s][:, :].rearrange("p (k n) -> p k n", n=512)
        for j in range(4):
            for r in range(2):
                src = self.attn_w_in[:, 2048 + j * 64:2048 + (j + 1) * 64].rearrange("(k p) n -> p k n", p=128)
                S.dma("pool", wv[:, :, j * 128 + r * 64:j * 128 + (r + 1) * 64], src, sem="w%d" % s,
                      writes=[("w", s, j * 2 + r)])
        for j in range(4):
            for (buf, key, t, off, w) in hsrc:
                b = self.rot("projps", 4)
                for k in range(KC):
                    S.op("pe", lambda e, j=j, k=k, b=b, buf=buf, off=off, w=w: e.matmul(
                        self.bank[b][:, 0:w], wv[:, k, j * 128:(j + 1) * 128], buf[:, k, off:off + w],
                        start=(k == 0), stop=(k == KC - 1)),
                        reads=self.wkeys(s) + [(key, k, t)], writes=[("ps", b)])
                ko = 0 if key == "hh" else 128 + off
                self.qknorm(b, w, self.vec[:, C_KG:C_KG + 1], "vec", self.kT[:, j, ko:ko + w], ("kT", j, ko))
        kT_keys = [("kT", j, ko) for j in range(4) for ko in (0, 128, 128 + TW)]
        s, wv = self.wload("kn", self.attn_w_in[:, 2304:2560], ncols=256)
        for blk in range(9):
            b = self.rot("projps", 4)
            for k in range(KC):
                if blk == 0:
                    lhs, key = self.hh[:, k, 0:128], ("hh", k, 0)
                else:
                    lhs, key = self.h[:, k, (blk - 1) * 128:blk * 128], ("h", k, (blk - 1) // 4)
                S.op("pe", lambda e, k=k, b=b, lhs=lhs, wv=wv: e.matmul(self.bank[b][:, 0:256], lhs, wv[:, k, :],
                                                                 start=(k == 0), stop=(k == KC - 1)),
                     reads=self.wkeys(s) + [key], writes=[("ps", b)])
            S.op("act", lambda e, b=b, blk=blk: e.activation(self.vtm[:, blk, :], self.bank[b][:, 0:256], AF.Identity),
                 reads=[("ps", b)], writes=[("vtm", blk)])
        PO, PD, PST = 4, 5, 7
        for g in range(4):
            s, wv = self.wload("kn", self.attn_w_in[:, g * 512:(g + 1) * 512])
            qi = self.rot("mid", 2)
            midq = self.mid[qi]
            for j in range(4):
                for t in range(NTT):
                    b = self.rot("projps", 4)
                    for k in range(KC):
                        S.op("pe", lambda e, j=j, t=t, k=k, b=b, wv=wv: e.matmul(
                            self.bank[b][:], wv[:, k, j * 128:(j + 1) * 128], self.h[:, k, t * TW:(t + 1) * TW],
                            start=(k == 0), stop=(k == KC - 1)),
                            reads=self.wkeys(s) + [("h", k, t)], writes=[("ps", b)])
                    self.qknorm(b, TW, self.qg8[:, 0:1], "qg8", midq[:, j, t * TW:(t + 1) * TW], ("mid", qi, j, t))
            oi = self.rot("mid", 2)
            mido = self.mid[oi]
            steps = [(j, half, r, kb) for j in range(4) for half in range(2) for r in range(2)
                     for kb in range(half * 4, half * 4 + 5)]
            LOOK = 2
            info = {}

            def front(i, g=g, midq=midq, qi=qi):
                j, half, r, kb = steps[i]
                n0 = half * 4
                head = 8 * g + 2 * j + r
                slope = float(2.0 ** (-8.0 * (head + 1) / 32.0))
                p0 = r * 64
                qlo = kb - 1 if kb - 1 >= n0 else kb
                qhi = kb if kb <= n0 + 3 else kb - 1
                c0 = 0 if qlo == kb - 1 else 128
                wd = (qhi - qlo + 1) * 128
                sti = (7, 6, 0, 1)[self.rot("st", 4)]
                st = self.bank[sti][:, 0:wd]
                S.op("pe", lambda e: e.matmul(st, self.kT[p0:p0 + 64, g, kb * 128:(kb + 1) * 128],
                                              midq[p0:p0 + 64, j, qlo * 128:(qhi + 1) * 128], start=True, stop=True),
                     reads=kT_keys + [("mid", qi, j, 0), ("mid", qi, j, 1)], writes=[("ps", sti)])
                ai = self.rot("tmpa", 3)
                S.op("dve", lambda e: e.scalar_tensor_tensor(self.tmpa[ai][:, 0:wd], self.dmat[:, c0:c0 + wd], slope, st,
                                                             ALU.mult, ALU.add),
                     reads=[("ps", sti), "dmat"], writes=[("tmpa", ai)])
                pi = self.rot("pT", 4)
                if kb == 0:
                    S.op("act", lambda e: e.activation(self.pT[pi][:, 0:wd], self.tmpa[ai][:, 0:wd], AF.Exp,
                                                       bias=self.vec[:, C_HB:C_HB + 1]),
                         reads=[("tmpa", ai), "vec"], writes=[("pT", pi)])
                else:
                    S.op("act", lambda e: e.activation(self.pT[pi][:, 0:wd], self.tmpa[ai][:, 0:wd], AF.Exp),
                         reads=[("tmpa", ai)], writes=[("pT", pi)])
                info[i] = (pi, qlo, qhi, p0, n0)

            def back(i, g=g, mido=mido, oi=oi):
                j, half, r, kb = steps[i]
                pi, qlo, qhi, p0, n0 = info.pop(i)
                for n in range(qlo, qhi + 1):
                    off = (n - qlo) * 128
                    oc = (n - n0) * 128
                    S.op("pe", lambda e, off=off, oc=oc, n=n: e.matmul(
                        self.bank[PO][p0:p0 + 64, oc:oc + 128], self.vtm[:, kb, g * 64:(g + 1) * 64],
                        self.pT[pi][:, off:off + 128], start=(kb == n), stop=(kb == n + 1)),
                        reads=[("pT", pi), ("vtm", kb)], writes=[("ps", PO)])
                    S.op("pe", lambda e, off=off, oc=oc, n=n: e.matmul(
                        self.bank[PD][p0:p0 + 64, oc:oc + 128], self.ones_bf[:, 0:64],
                        self.pT[pi][:, off:off + 128], start=(kb == n), stop=(kb == n + 1)),
                        reads=[("pT", pi), "ones_bf"], writes=[("ps", PD)])
                if r == 1 and kb == n0 + 4:
                    f = self.rot("hf", self.NHF)
                    ch = 4 * g + j
                    S.op("dve", lambda e: e.tensor_scalar(self.hf[f][:], self.bank[PD][:], self.esink[:, ch:ch + 1], None, ALU.add),
                         reads=[("ps", PD), "esink"], writes=[("hf", f)])
                    S.op("dve", lambda e: e.reciprocal(self.hf[f][:], self.hf[f][:]), reads=[("hf", f)], writes=[("hf", f)])
                    S.op("dve", lambda e: e.tensor_tensor(mido[:, j, half * TW:(half + 1) * TW], self.bank[PO][:], self.hf[f][:], ALU.mult),
                         reads=[("ps", PO), ("hf", f)], writes=[("mid", oi, j, half)])

            for i in range(min(LOOK, len(steps))):
                front(i)
            for i in range(len(steps)):
                if i + LOOK < len(steps):
                    front(i + LOOK)
                back(i)
            self.stage2(l, 32, self.attn_w_out[g * 512:(g + 1) * 512, :], oi)

    def barrier(self, keys):
        self.S.op("dve", lambda e: e.memset(self.bar[:], 0.0), reads=[], writes=list(keys) + ["bar"])

    def hgrn_consts(self):
        S = self.S
        S.dma("sp", self.cst2[:], self.cst2_d, sem="cst2", writes=["cst2"])
        S.op("dve", lambda e: e.tensor_copy(self.identb[:], self.cst2[:, 64:192]), reads=["cst2"], writes=["identb"])
        S.op("dve", lambda e: e.tensor_tensor(self.lbv[:, 0:16], self.vec[:, C_LB + 16:C_LB + 32],
                                              self.vec[:, C_LB:C_LB + 16], ALU.subtract), reads=["vec"], writes=["lbv0"])
        S.op("act", lambda e: e.activation(self.lbv[:, 0:16], self.lbv[:, 0:16], AF.Sigmoid), reads=["lbv0"], writes=["lbv0"])
        S.op("dve", lambda e: e.tensor_scalar(self.lbv[:, 16:32], self.lbv[:, 0:16], -1.0, 1.0, ALU.mult, ALU.add),
             reads=["lbv0"], writes=["lbv1"])
        S.op("dve", lambda e: e.tensor_scalar(self.lbv[:, 32:48], self.lbv[:, 0:16], 1.0, -1.0, ALU.mult, ALU.add),
             reads=["lbv0"], writes=["lbv2"])

    def hgrn_zero_state(self):
        self.S.op("dve", lambda e: e.memset(self.Sf[:], 0.0), writes=[("Sf", hd) for hd in range(16)])

    def hgrn_dtot(self):
        S = self.S
        lt = self.lsum[:, :].rearrange("p (h t) -> p h t", t=2)
        S.op("dve", lambda e: e.tensor_tensor(self.dtot[:, :].rearrange("p (h t) -> p h t", t=1), lt[:, :, 0:1], lt[:, :, 1:2], ALU.add),
             reads=[("lsum", i) for i in range(32)], writes=["dtot"])
        S.op("act", lambda e: e.activation(self.dtot[:, :], self.dtot[:, :], AF.Exp), reads=["dtot"], writes=["dtot"])

    def hgrn_chain(self, u_src, dall_src):
        S = self.S
        S.dma("sp", self.dall[:, 0:(NCORES - 1) * 16].rearrange("p (r w) -> p r w", w=16), dall_src, sem="dall", reads=["gath"], writes=["dall"])
        for jc in range(NCORES - 1):
            mj = self.vec[:, C_MS + jc:C_MS + jc + 1]
            omj = self.vec[:, C_OMS + jc:C_OMS + jc + 1]
            S.op("dve", lambda e, jc=jc, mj=mj, omj=omj: e.tensor_scalar(
                self.avec[:], self.dall[:, jc * 16:(jc + 1) * 16], mj, omj, ALU.mult, ALU.add),
                reads=["dall", "vec"], writes=["avec"])
            for pc in range(4):
                ui = self.rot("hf", self.NHF)
                ub = self.hf[ui]
                S.dma("sp", ub[:], u_src(jc, pc), sem="ust%d" % ui, reads=["gath"], writes=[("hf", ui)])
                S.op("dve", lambda e, mj=mj, ub=ub: e.tensor_scalar(ub[:], ub[:], mj, None, ALU.mult),
                     reads=[("hf", ui), "vec"], writes=[("hf", ui)])
                for hh_ in range(4):
                    hd = pc * 4 + hh_
                    S.op("dve", lambda e, hd=hd, hh_=hh_, ub=ub: e.scalar_tensor_tensor(
                        self.Sf[:, hd, :], self.Sf[:, hd, :], self.avec[:, hd:hd + 1], ub[:, hh_ * 128:(hh_ + 1) * 128],
                        ALU.mult, ALU.add), reads=[("Sf", hd), "avec", ("hf", ui)], writes=[("Sf", hd)])

    def hgrn_exchange(self):
        S = self.S
        nc = self.nc
        WB = 2048 + 16
        bounce = nc.dram_tensor("hg_bounce", [128, WB], F32)
        gath = nc.dram_tensor("hg_gath", [NCORES * 128, WB], F32, addr_space="Shared")
        S.dma("sp", bounce[:, 0:2048], self.Sf[:, :, :].rearrange("p h v -> p (h v)"), sem="bnc",
              reads=[("Sf", hd) for hd in range(16)], writes=["bounceU"])
        S.dma("sp", bounce[:, 2048:WB], self.dtot[:, :], sem="bnc2", reads=["dtot"], writes=["bounceD"])
        o = S.dma("pool", None, None, sem="cc", reads=["bounceU", "bounceD"], writes=["gath"], inc=1)
        o.fn = lambda e: e.collective_compute("AllGather", ALU.bypass, replica_groups=[list(range(NCORES))],
                                              ins=[bounce.ap().opt()], outs=[gath.ap().opt()])
        g3 = gath.ap().rearrange("(r p) w -> p r w", p=128)
        self.hgrn_zero_state()
        self._gath_key = "gath"
        self.hgrn_chain(lambda jc, pc: g3[:, jc, pc * 512:(pc + 1) * 512], g3[:, 0:NCORES - 1, 2048:WB])

    def hgrn(self):
        S = self.S
        l = 1
        if self.hpass == 0:
            self.hgrn_consts()
            self.prenorm(l, 0)
            self.hgrn_pass1()
            self.hgrn_exchange()
            self.hgrn_pass(True)
        elif self.hpass == 1:
            self.hgrn_consts()
            self.prenorm(l, 0)
            self.hgrn_pass1()
            S.dma("sp", self.dt_d, self.dtot[:, :], sem="st", reads=["dtot"])
            S.dma("sp", self.U_d, self.Sf[:, :, :].rearrange("p h v -> p (h v)"), sem="st", reads=[("Sf", hd) for hd in range(16)])
        else:
            self.hgrn_consts()
            self.prenorm(l, 0)
            self.hgrn_zero_state()
            d3 = self.dall_d.rearrange("p (r w) -> p r w", w=16)
            self.hgrn_chain(lambda jc, pc: self.Uall_d[jc, :, pc * 512:(pc + 1) * 512], d3[:, 0:NCORES - 1, :])
            self.hgrn_pass(True)

    def hgrn_pass1(self):
        S = self.S
        W = self.hgrn_w_in
        lb, oml, noml = self.lbv[:, 0:16], self.lbv[:, 16:32], self.lbv[:, 32:48]
        lbk = ["lbv0", "lbv1", "lbv2"]
        ones_b = self.ones_bf[:, 0:1].to_broadcast([128, TW])
        for hd in range(16):
            sw = self.slot_i % NSLOT
            self.slot_i += 1
            wh = self.wslot[sw][:, :].rearrange("p (k n) -> p k n", n=512)
            for qi_, cb in ((0, 2048), (3, 4096)):
                S.dma("pool", wh[:, :, qi_ * 128:(qi_ + 1) * 128],
                      W[:, cb + hd * 128:cb + (hd + 1) * 128].rearrange("(k p) n -> p k n", p=128),
                      sem="w%d" % sw, writes=[("w", sw, 2 * qi_), ("w", sw, 2 * qi_ + 1)])
            for blk in range(8):
                b = self.rot("projps3", 3)
                for k in range(KC):
                    S.op("pe", lambda e, k=k, b=b, blk=blk, wh=wh: e.matmul(
                        self.bank[b][:, 0:128], self.h[:, k, blk * 128:(blk + 1) * 128], wh[:, k, 384:512],
                        start=(k == 0), stop=(k == KC - 1)),
                        reads=self.wkeys(sw) + [("h", k, blk // 4)], writes=[("ps", b)])
                S.op("act", lambda e, b=b, blk=blk: e.activation(self.vth[:, blk, :], self.bank[b][:, 0:128], AF.Identity),
                     reads=[("ps", b)], writes=[("vth", blk)])
            tk, tb = [], []
            for t in range(NTT):
                b = self.rot("projps3", 3)
                for k in range(KC):
                    S.op("pe", lambda e, k=k, b=b, t=t, wh=wh: e.matmul(self.bank[b][:], wh[:, k, 0:128],
                                                                 self.h[:, k, t * TW:(t + 1) * TW], start=(k == 0), stop=(k == KC - 1)),
                         reads=self.wkeys(sw) + [("h", k, t)], writes=[("ps", b)])
                i_sg = self.rot("hf", self.NHF)
                S.op("act", lambda e, o=self.hf[i_sg], b=b: e.activation(o[:], self.bank[b][:], AF.Sigmoid),
                     reads=[("ps", b)], writes=[("hf", i_sg)])
                i_lf = self.rot("hf", self.NHF)
                S.op("act", lambda e, o=self.hf[i_lf], i=self.hf[i_sg], hd=hd: e.activation(
                    o[:], i[:], AF.Ln, scale=oml[:, hd:hd + 1], bias=lb[:, hd:hd + 1]),
                    reads=[("hf", i_sg)] + lbk, writes=[("hf", i_lf)])
                i_kk = self.rot("hf", self.NHF)
                S.op("dve", lambda e, o=self.hf[i_kk], i=self.hf[i_sg], hd=hd: e.tensor_scalar(
                    o[:], i[:], noml[:, hd:hd + 1], oml[:, hd:hd + 1], ALU.mult, ALU.add),
                    reads=[("hf", i_sg)] + lbk, writes=[("hf", i_kk)])
                i_b = self.rot("hf", self.NHF)
                init = 0.0 if t == 0 else self.hf[tb[0]][:, TW - 1:TW]
                rd = [("hf", i_lf), "ones_bf"] + ([("hf", tb[0])] if t else [])
                S.op("dve", lambda e, o=self.hf[i_b], i=self.hf[i_lf], init=init: e.tensor_tensor_scan(
                    o[:], ones_b, i[:], init, ALU.mult, ALU.add), reads=rd, writes=[("hf", i_b)])
                tk.append(i_kk)
                tb.append(i_b)
            bend = self.hf[tb[1]][:, TW - 1:TW]
            S.op("act", lambda e, hd=hd, bend=bend: e.activation(self.dtot[:, hd:hd + 1], bend, AF.Exp),
                 reads=[("hf", tb[1])], writes=["dtot"])
            for t in range(NTT):
                S.op("act", lambda e, o=self.hf[tb[t]], bend=bend: e.activation(o[:], o[:], AF.Exp, scale=-1.0, bias=bend),
                     reads=[("hf", tb[t]), ("hf", tb[1])], writes=[("hf", tb[t])] if t == 0 else [("hf", tb[1])])
                i_kl = self.rot("hb", 6)
                S.op("dve", lambda e, o=self.hb[i_kl], a=self.hf[tk[t]], b_=self.hf[tb[t]]: e.tensor_tensor(o[:], a[:], b_[:], ALU.mult),
                     reads=[("hf", tk[t]), ("hf", tb[t])], writes=[("hb", i_kl)])
                for q4 in range(4):
                    blk = t * 4 + q4
                    S.op("pe", lambda e, blk=blk, q4=q4, kl=self.hb[i_kl]: e.transpose(
                        self.pst[:, blk * 128:(blk + 1) * 128], kl[:, q4 * 128:(q4 + 1) * 128], self.identb[:]),
                        reads=[("hb", i_kl), "identb"], writes=[("ps", 7)])
            S.op("act", lambda e: e.activation(self.klT8[:, :, :].rearrange("p a b -> p (a b)"), self.pst[:, 0:1024], AF.Identity),
                 reads=[("ps", 7)], writes=["klT8"])
            for blk in range(8):
                S.op("pe", lambda e, blk=blk: e.matmul(self.bank[5][:, 0:128], self.klT8[:, blk, :], self.vth[:, blk, :],
                                                       start=(blk == 0), stop=(blk == 7)),
                     reads=["klT8", ("vth", blk)], writes=[("ps", 5)])
            S.op("act", lambda e, hd=hd: e.activation(self.Sf[:, hd, :], self.bank[5][:, 0:128], AF.Identity),
                 reads=[("ps", 5)], writes=[("Sf", hd)])

    def hgrn_pass(self, p2):
        S = self.S
        l = 1
        W = self.hgrn_w_in
        cm64 = self.cst2[:, 0:64]
        smask = self.cst2[:, 192:704]
        lb, oml, noml = self.lbv[:, 0:16], self.lbv[:, 16:32], self.lbv[:, 32:48]
        lbk = ["lbv0", "lbv1", "lbv2"]
        PSO = 4

        def proj(wv, s, j, t):
            b = self.rot("projps3", 3)
            for k in range(KC):
                S.op("pe", lambda e, k=k: e.matmul(self.bank[b][:], wv[:, k, :],
                                                   self.h[:, k, t * TW:(t + 1) * TW], start=(k == 0), stop=(k == KC - 1)),
                     reads=self.wkeys(s) + [("h", k, t)], writes=[("ps", b)])
            return b

        def hf():
            i = self.rot("hf", self.NHF)
            return self.hf[i], ("hf", i)

        def hb():
            i = self.rot("hb", 6)
            return self.hb[i], ("hb", i)

        for hg in range(4):
            if p2:
                midi = self.rot("mid", 2)
                mid = self.mid[midi]
            for j in range(4):
                hd = 4 * hg + j
                sw = self.slot_i % NSLOT
                self.slot_i += 1
                wh = self.wslot[sw][:, :].rearrange("p (k n) -> p k n", n=512)
                for qi_, cb in enumerate((2048, 0, 6144, 4096)):
                    if (not p2) and qi_ in (1, 2):
                        continue
                    S.dma("pool", wh[:, :, qi_ * 128:(qi_ + 1) * 128],
                          W[:, cb + hd * 128:cb + (hd + 1) * 128].rearrange("(k p) n -> p k n", p=128),
                          sem="w%d" % sw, writes=[("w", sw, 2 * qi_), ("w", sw, 2 * qi_ + 1)])
                wvf, wvq, wvg = wh[:, :, 0:128], wh[:, :, 128:256], wh[:, :, 256:384]
                sf = sq_ = sw
                self._wg = (sw, wvg)
                for blk in range(8):
                    b = self.rot("projps3", 3)
                    for k in range(KC):
                        S.op("pe", lambda e, k=k, b=b, blk=blk, wh=wh: e.matmul(
                            self.bank[b][:, 0:128], self.h[:, k, blk * 128:(blk + 1) * 128], wh[:, k, 384:512],
                            start=(k == 0), stop=(k == KC - 1)),
                            reads=self.wkeys(sw) + [("h", k, blk // 4)], writes=[("ps", b)])
                    S.op("act", lambda e, b=b, blk=blk: e.activation(self.vth[:, blk, :], self.bank[b][:, 0:128], AF.Identity),
                         reads=[("ps", b)], writes=[("vth", blk)])
                if p2 and j == 0:
                    pass
                for t in range(NTT):
                    bf_ = proj(wvf, sf, j, t)
                    t_sg, k_sg = hf()
                    S.op("act", lambda e, o=t_sg, b=bf_: e.activation(o[:], self.bank[b][:], AF.Sigmoid),
                         reads=[("ps", bf_)], writes=[k_sg])
                    t_lf, k_lf = hf()
                    S.op("act", lambda e, o=t_lf, i=t_sg, hd=hd: e.activation(o[:], i[:], AF.Ln, scale=oml[:, hd:hd + 1],
                                                                        bias=lb[:, hd:hd + 1]),
                         reads=[k_sg] + lbk, writes=[k_lf])
                    t_kk, k_kk = hf()
                    S.op("dve", lambda e, o=t_kk, i=t_sg, hd=hd: e.tensor_scalar(o[:], i[:], noml[:, hd:hd + 1], oml[:, hd:hd + 1],
                                                                           ALU.mult, ALU.add),
                         reads=[k_sg] + lbk, writes=[k_kk])
                    t_b, k_b = hf()
                    S.op("dve", lambda e, o=t_b, i=t_lf: e.tensor_tensor_scan(o[:], smask, i[:], 0.0, ALU.mult, ALU.add),
                         reads=[k_lf, "cst2"], writes=[k_b])
                    b3 = t_b[:, :].rearrange("p (c s) -> p c s", s=64)
                    t_d, k_d = hf()
                    S.op("dve", lambda e, o=t_d, b3=b3: e.tensor_tensor(o[:, :].rearrange("p (c s) -> p c s", s=64),
                                                                  b3[:, :, 63:64].to_broadcast([128, 8, 64]), b3, ALU.subtract),
                         reads=[k_b], writes=[k_d])
                    S.op("act", lambda e, o=t_d: e.activation(o[:], o[:], AF.Exp), reads=[k_d], writes=[k_d])
                    t_kl, k_kl = hb()
                    S.op("dve", lambda e, o=t_kl, a=t_kk, b_=t_d: e.tensor_tensor(o[:], a[:], b_[:], ALU.mult),
                         reads=[k_kk, k_d], writes=[k_kl])
                    di = self.rot("dec", 2)
                    dec = self.dec[di]
                    S.op("act", lambda e, dec=dec, b3=b3: e.activation(dec[:, :].rearrange("p (c s) -> p c s", s=1),
                                                                   b3[:, :, 63:64], AF.Exp),
                         reads=[k_b], writes=[("dec", di)])
                    S.op("dve", lambda e, b3=b3, hd=hd, t=t: e.tensor_reduce(
                        self.lsum[:, hd * 2 + t:hd * 2 + t + 1], b3[:, :, 63:64].rearrange("p c s -> p (c s)"),
                        mybir.AxisListType.X, ALU.add), reads=[k_b], writes=[("lsum", hd * 2 + t)])
                    if p2:
                        bq_ = proj(wvq, sq_, j, t)
                        t_qs, k_qs = hf()
                        S.op("act", lambda e, o=t_qs, b=bq_: e.activation(o[:], self.bank[b][:], AF.Silu),
                             reads=[("ps", bq_)], writes=[k_qs])
                        t_e, k_e = hf()
                        S.op("dve", lambda e, o=t_e, b3=b3: e.tensor_tensor(o[:, :].rearrange("p (c s) -> p c s", s=64), b3,
                                                                      b3[:, :, 31:32].to_broadcast([128, 8, 64]), ALU.subtract),
                             reads=[k_b], writes=[k_e])
                        t_e2, k_e2 = hf()
                        S.op("act", lambda e, o=t_e2, i=t_e: e.activation(o[:], i[:], AF.Exp, scale=-1.0),
                             reads=[k_e], writes=[k_e2])
                        S.op("act", lambda e, o=t_e: e.activation(o[:], o[:], AF.Exp), reads=[k_e], writes=[k_e])
                        t_qe, k_qe = hb()
                        S.op("dve", lambda e, o=t_qe, q=t_qs, x_=t_e: e.scalar_tensor_tensor(o[:], q[:], HSCALE, x_[:],
                                                                                       ALU.mult, ALU.mult),
                             reads=[k_qs, k_e], writes=[k_qe])
                        t_ke, k_ke = hb()
                        S.op("dve", lambda e, o=t_ke, a=t_kk, x_=t_e2: e.tensor_tensor(o[:], a[:], x_[:], ALU.mult),
                             reads=[k_kk, k_e2], writes=[k_ke])
                        S.op("act", lambda e, o=t_e2, i=t_b: e.activation(o[:], i[:], AF.Exp), reads=[k_b], writes=[k_e2])
                        t_qb, k_qb = hb()
                        S.op("dve", lambda e, o=t_qb, q=t_qs, x_=t_e2: e.scalar_tensor_tensor(o[:], q[:], HSCALE, x_[:],
                                                                                        ALU.mult, ALU.mult),
                             reads=[k_qs, k_e2], writes=[k_qb])
                    for q4 in range(4):
                        S.op("pe", lambda e, q4=q4, kl=t_kl: e.transpose(self.pst[:, q4 * 128:(q4 + 1) * 128],
                                                                  kl[:, q4 * 128:(q4 + 1) * 128], self.identb[:]),
                             reads=[k_kl, "identb"], writes=[("ps", 7)])
                    S.op("act", lambda e: e.activation(self.klT4[:, :, :].rearrange("p a b -> p (a b)"), self.pst[:, 0:512], AF.Identity),
                         reads=[("ps", 7)], writes=["klT4"])
                    for c in range(8):
                        par = c % 2
                        blk = (t * 8 + c) // 2
                        pr = slice(par * 64, par * 64 + 64)
                        ub = 5 + c % 2
                        S.op("pe", lambda e, c=c, pr=pr, blk=blk, ub=ub: e.matmul(
                            self.bank[ub][:, (c // 2) * 128:(c // 2 + 1) * 128], self.klT4[pr, c // 2, :], self.vth[pr, blk, :],
                            start=True, stop=True), reads=["klT4", ("vth", blk)], writes=[("ps", ub)])
                    for c in range(8):
                        par = c % 2
                        pr = slice(par * 64, par * 64 + 64)
                        cc = slice(c * 64, c * 64 + 64)
                        S.op("pe", lambda e, pr=pr, cc=cc, ke=t_ke, qe=t_qe: e.matmul(
                            self.bank[3][pr, cc], ke[:, cc], qe[:, cc], start=True, stop=True),
                            reads=[k_ke, k_qe], writes=[("ps", 3)])
                    for c in range(8):
                        par = c % 2
                        pr = slice(par * 64, par * 64 + 64)
                        cc = slice(c * 64, c * 64 + 64)
                        S.op("dve", lambda e, pr=pr, c=c, cc=cc: e.tensor_tensor(
                            self.At8[pr, c, :], self.bank[3][pr, cc], cm64[pr, :], ALU.mult),
                            reads=[("ps", 3), "cst2"], writes=[("At8", c)])
                    bufs = [(self.Sf[:, hd, :], ("Sf", hd)), (self.Stmp[:, :], "Stmp")]
                    for c in range(8):
                        (src, ksrc), (dst, kdst) = bufs[c % 2], bufs[(c + 1) % 2]
                        ub = 5 + c % 2
                        S.op("act", lambda e, c=c, src=src: e.activation(self.Sb8[:, c, :], src, AF.Identity),
                             reads=[ksrc], writes=[("Sb8", c)])
                        S.op("dve", lambda e, c=c, src=src, dst=dst, ub=ub, dec=dec: e.scalar_tensor_tensor(
                            dst, src, dec[:, c:c + 1], self.bank[ub][:, (c // 2) * 128:(c // 2 + 1) * 128], ALU.mult, ALU.add),
                            reads=[ksrc, ("dec", di), ("ps", ub)], writes=[kdst])
                    for c in range(8):
                        par = c % 2
                        blk = (t * 8 + c) // 2
                        pr = slice(par * 64, par * 64 + 64)
                        cc = slice(c * 64, c * 64 + 64)
                        S.op("pe", lambda e, c=c, pr=pr, cc=cc, blk=blk: e.matmul(
                            self.bank[PSO][:, cc], self.vth[pr, blk, :], self.At8[pr, c, :],
                            start=True, stop=False), reads=[("At8", c), ("vth", blk)], writes=[("ps", PSO)])
                        S.op("pe", lambda e, c=c, cc=cc, qb=t_qb: e.matmul(
                            self.bank[PSO][:, cc], self.Sb8[:, c, :], qb[:, cc], start=False, stop=True),
                            reads=[("Sb8", c), k_qb], writes=[("ps", PSO)])
                    if p2:
                        bg_ = None
                        sg_, wvg = self._wg
                        bg_ = proj(wvg, sg_, j, t)
                        t_gs, k_gs = hf()
                        S.op("act", lambda e, o=t_gs, b=bg_: e.activation(o[:], self.bank[b][:], AF.Silu),
                             reads=[("ps", bg_)], writes=[k_gs])
                        q = self.rot("sq", 2)
                        S.op("act", lambda e, q=q: e.activation(self.sq[q][:], self.bank[PSO][:], AF.Square),
                             reads=[("ps", PSO)], writes=[("sq", q)])
                        S.op("pe", lambda e, q=q: e.matmul(self.bank[6][:], self.ones_bf[:], self.sq[q][:], start=True, stop=True),
                             reads=[("sq", q), "ones_bf"], writes=[("ps", 6)])
                        r = self.rot("rstd", 2)
                        S.op("act", lambda e, r=r: e.activation(self.rstd[r][:], self.bank[6][:], AF.Sqrt,
                                                              bias=self.epsv[:, 0:1], scale=1.0 / 128),
                             reads=[("ps", 6), "epsv"], writes=[("rstd", r)])
                        S.op("dve", lambda e, r=r: e.reciprocal(self.rstd[r][:], self.rstd[r][:]),
                             reads=[("rstd", r)], writes=[("rstd", r)])
                        t_o, k_o = hf()
                        S.op("dve", lambda e, o=t_o, r=r, hd=hd: e.scalar_tensor_tensor(
                            o[:], self.bank[PSO][:], self.vec[:, C_OG + hd:C_OG + hd + 1], self.rstd[r][:], ALU.mult, ALU.mult),
                            reads=[("ps", PSO), ("rstd", r), "vec"], writes=[k_o])
                        S.op("dve", lambda e, o=t_o, g_=t_gs, j=j, t=t, mid=mid: e.tensor_tensor(
                            mid[:, j, t * TW:(t + 1) * TW], o[:], g_[:], ALU.mult),
                            reads=[k_o, k_gs], writes=[("mid", midi, j, t)])
            if p2:
                self.stage2(l, 32, self.hgrn_w_out[hg * 512:(hg + 1) * 512, :], midi)

    def mlp(self, l, between=None):
        S = self.S
        self.prenorm(l, 1)
        for fg in range(DFF // 512):
            if between is not None:
                between(fg)
            s, wv = self.wload("kn", self.mlp_w1[l, :, fg * 512:(fg + 1) * 512])
            midi = self.rot("mid", 2)
            mid = self.mid[midi]
            for j in range(4):
                for t in range(NTT):
                    b = self.rot("projps", 4)
                    for k in range(KC):
                        S.op("pe", lambda e, j=j, t=t, k=k, b=b, wv=wv: e.matmul(
                            self.bank[b][:], wv[:, k, j * 128:(j + 1) * 128], self.h[:, k, t * TW:(t + 1) * TW],
                            start=(k == 0), stop=(k == KC - 1)),
                            reads=self.wkeys(s) + [("h", k, t)], writes=[("ps", b)])
                    f = self.rot("hf", self.NHF)
                    S.op("act", lambda e, b=b, f=f: e.activation(self.hf[f][:], self.bank[b][:], AF.Relu),
                         reads=[("ps", b)], writes=[("hf", f)])
                    S.op("pool", lambda e, j=j, t=t, f=f, mid=mid: e.tensor_tensor(
                        mid[:, j, t * TW:(t + 1) * TW], self.hf[f][:], self.hf[f][:], ALU.mult),
                        reads=[("hf", f)], writes=[("mid", midi, j, t)])
            self.stage2(l, 80, self.mlp_w2[l, fg * 512:(fg + 1) * 512, :], midi)

    def epilogue(self):
        S = self.S
        os_ = self.out_d.rearrange("(k p) t -> p k t", p=128)
        for c in range(KC):
            S.dma("sp", os_[:, c, :], self.x[:, c, :], sem="st", reads=[("x", c, 0), ("x", c, 1)])
        S.finish(final_waits=["st"])


    def build_fused(self):
        self.mod_blocks(0, range(0, 8))
        self.mod_coef_part(0, "a1b1")
        self.prenorm(0, 0)
        self.prenorm(0, 0, src=self.xh, dst=self.hh, ntok=128, keyx="xh", keyh="hh")
        self.mod_blocks(0, range(8, 24))
        for part in ("g1", "a2b2", "g2"):
            self.mod_coef_part(0, part)
        self.attention(do_prenorm=False)
        sched = {fg: (range(2 * fg, 2 * fg + 2) if fg < 8 else range(8 + fg, 9 + fg)) for fg in range(16)}

        def between(fg):
            if fg > 0:
                self.mod_blocks(1, sched[fg - 1])

        self.mlp(0, between=between)
        self.mod_blocks(1, sched[15])
        for part in ("a1b1", "g1", "a2b2", "g2"):
            self.mod_coef_part(1, part)
        self.barrier(self.attn_keys + self.hgrn_keys)
        self.hgrn()
        self.mlp(1)
        self.epilogue()
        return self.nc

    def build(self):
        self.prologue()
        if self.fused:
            return self.build_fused()
        for l in self.layers:
            self.mod_layer(l)
        for l in self.layers:
            if l == 0:
                self.attention()
            else:
                if 0 in self.layers:
                    self.barrier(self.attn_keys + self.hgrn_keys)
                self.hgrn()
                if self.hpass == 1:
                    self.S.finish(final_waits=["st"])
                    return self.nc
            if self.debug != "nomlp":
                self.mlp(l)
        self.epilogue()
        return self.nc


def _dummy():
    pass


def host_inputs(inputs, layers=(0, 1), x_override=None, debug=None, hpass=2, Uall=None, dall=None):
    f32 = np.float32
    x = np.asarray(inputs["x"], f32)[0] if x_override is None else x_override
    xT = np.ascontiguousarray(x.T)
    s_idx = np.arange(128)[:, None]
    t_idx = np.arange(256)[None, :]
    dist = t_idx - s_idx
    dmat = np.where((dist >= 0) & (dist < 128), -dist, NEG).astype(f32)
    cst2 = np.zeros((128, 64 + 128 + 512), f32)
    cst2[:, 0:64] = ((np.arange(128)[:, None] % 64) <= np.arange(64)[None, :])
    cst2[:, 64:192] = np.eye(128)
    cst2[:, 192:704] = 1.0
    cst2[:, 192:704:64] = 0.0
    base = np.zeros((128, NV), f32)
    base[:, C_C:C_C + 16] = _fm(inputs["c"][0])
    for l in range(2):
        base[:, C_L[l]:C_L[l] + 16] = _fm(inputs["norm_mix"][l])
        base[:, C_L[l] + 16:C_L[l] + 32] = _fm(inputs["norm_mlp"][l])
        base[:, C_L[l] + 32:C_L[l] + 128] = _fm(inputs["mod_b"][l])
        base[:, C_LB + 16 * l:C_LB + 16 * (l + 1)] = _fm(inputs["hgrn_lb_logits"][l])
    base[:, C_OG:C_OG + 16] = np.asarray(inputs["hgrn_o_gain"][0], f32).T
    base[:, C_QG] = np.tile(np.asarray(inputs["attn_q_gain"][0], f32), 2)
    base[:, C_KG] = np.tile(np.asarray(inputs["attn_k_gain"][0], f32), 2)
    sk = np.asarray(inputs["attn_sinks"][0], f32)
    base[:, C_SINK:C_SINK + 16] = sk[(2 * np.arange(16)[None, :] + (np.arange(128)[:, None] // 64))]
    maps = []
    for i in range(NCORES):
        v = base.copy()
        v[:, C_HB] = NEG if i == 0 else 0.0
        for j in range(NCORES):
            v[:, C_MS + j] = 1.0 if j < i else 0.0
            v[:, C_OMS + j] = 0.0 if j < i else 1.0
        m = {"xT": np.ascontiguousarray(xT[:, i * NT:(i + 1) * NT]), "vecs": v,
             "mod_w": np.ascontiguousarray(np.asarray(inputs["mod_w"], f32)[list(layers)])}
        if debug != "nomlp" and not (1 in layers and hpass == 1):
            m["mlp_w1"] = np.asarray(inputs["mlp_w1"], f32)
            m["mlp_w2"] = np.asarray(inputs["mlp_w2"], f32)
        if 0 in layers:
            m["xh"] = np.zeros((D, 128), f32) if i == 0 else np.ascontiguousarray(xT[:, i * NT - 128:i * NT])
            m["dmat"] = dmat
            m["attn_w_in"] = np.asarray(inputs["attn_w_in"][0], f32)
            m["attn_w_out"] = np.asarray(inputs["attn_w_out"][0], f32)
        if 1 in layers:
            m["cst2"] = cst2
            m["hgrn_w_in"] = np.asarray(inputs["hgrn_w_in"][0], f32)
            if hpass != 1:
                m["hgrn_w_out"] = np.asarray(inputs["hgrn_w_out"][0], f32)
            if hpass == 2:
                m["Uall"] = Uall
                m["dall"] = dall
        maps.append(m)
    return maps


def run_layers(inputs, layers=(0, 1), x_override=None, trace=False, debug=None, hpass=2, Uall=None, dall=None):
    nc = Builder(layers=layers, debug=debug, hpass=hpass).build()
    maps = host_inputs(inputs, layers, x_override, debug, hpass, Uall, dall)
    res = run_bass_kernel_spmd(nc, maps, core_ids=list(range(NCORES)), trace=trace)
    if 1 in layers and hpass == 1:
        U = np.ascontiguousarray(np.stack([r["U_out"] for r in res.results], axis=0))
        dt = np.ascontiguousarray(np.concatenate([r["dt_out"] for r in res.results], axis=1))
        return (U, dt), res
    outT = np.concatenate([r["outT"] for r in res.results], axis=1)
    return np.ascontiguousarray(outT.T)[None].astype(np.float32), res


def kernel_unfused(**inputs):
    x1, _ = run_layers(inputs, layers=(0,))
    (U, dt), _ = run_layers(inputs, layers=(1,), x_override=x1[0], hpass=1)
    out, _ = run_layers(inputs, layers=(1,), x_override=x1[0], hpass=2, Uall=U, dall=dt)
    return out


def kernel(**inputs):
    out, _ = run_layers(inputs, layers=(0, 1), hpass=0)
    return out
```
